# Optimizing a Trainium2 kernel written in Bass

```python
import jax, jax.numpy as jnp
from jax import lax
import numpy as np

D_MODEL = 1024
BATCH = 8
SEQ = 4096
DEPTH = 4

N_MIXERS = 2
EXPAND = 2
D_BRANCH = EXPAND * D_MODEL
CHUNK = 128
GMLP_GROUPS = 8
GMLP_GROUP_W = D_BRANCH // GMLP_GROUPS
LRU_HEADS = 8
LRU_HEAD_W = D_BRANCH // LRU_HEADS
CONV_WIDTH = 4
LRU_C = 8.0
N_A = (DEPTH + 1) // 2
N_B = DEPTH // 2
EPS = 1e-6

kernel_name = "hybrid_gmlp_rglru_sandwich_adaln"


def rms_norm(x, g):
    xf = x.astype(jnp.float32)
    y = xf * lax.rsqrt(jnp.mean(xf * xf, axis=-1, keepdims=True) + EPS)
    return (y * g.astype(jnp.float32)).astype(x.dtype)


def layer_norm(x, g):
    xf = x.astype(jnp.float32)
    xc = xf - jnp.mean(xf, axis=-1, keepdims=True)
    y = xc * lax.rsqrt(jnp.mean(xc * xc, axis=-1, keepdims=True) + EPS)
    return (y * g.astype(jnp.float32)).astype(x.dtype)


def gmlp_mixer(h, w_in, v_norm, w_s, b_s, w_out):
    b, s, _ = h.shape
    z = h @ w_in
    u, v, g = jnp.split(z, 3, axis=-1)
    u = jax.nn.gelu(u)
    v = layer_norm(jax.nn.gelu(v), v_norm)
    n_chunks = s // CHUNK
    v = v.reshape(b, n_chunks, CHUNK, GMLP_GROUPS, GMLP_GROUP_W)
    causal = jnp.tril(jnp.ones((CHUNK, CHUNK), dtype=bool))
    w = jnp.where(causal[None], w_s, jnp.zeros_like(w_s))
    mixed = jnp.einsum('gts,bnsgc->bntgc', w, v) + jnp.transpose(b_s)[None, None, :, :, None]
    y = u * mixed.reshape(b, s, D_BRANCH) * jax.nn.silu(g)
    return y @ w_out


def _lru_combine(left, right):
    a1, b1 = left
    a2, b2 = right
    return a1 * a2, a2 * b1 + b2


def rglru_mixer(h, w_in, conv_w, conv_b, ga_w, ga_b, gx_w, gx_b, lam, w_out):
    b, s, _ = h.shape
    z = h @ w_in
    xb, g = jnp.split(z, 2, axis=-1)
    xp = jnp.pad(xb, ((0, 0), (CONV_WIDTH - 1, 0), (0, 0)))
    xc = conv_b + xp[:, 0:s] * conv_w[0]
    for k in range(1, CONV_WIDTH):
        xc = xc + xp[:, k:k + s] * conv_w[k]
    xh = xc.reshape(b, s, LRU_HEADS, LRU_HEAD_W)
    r = jax.nn.sigmoid(jnp.einsum('bshi,hij->bshj', xh, ga_w).reshape(b, s, D_BRANCH) + ga_b)
    i = jax.nn.sigmoid(jnp.einsum('bshi,hij->bshj', xh, gx_w).reshape(b, s, D_BRANCH) + gx_b)
    log_a = -LRU_C * r.astype(jnp.float32) * jax.nn.softplus(-lam.astype(jnp.float32))
    a = jnp.exp(log_a)
    mult = jnp.sqrt(-jnp.expm1(2.0 * log_a))
    bterm = mult * (i * xc).astype(jnp.float32)
    _, hs = lax.associative_scan(_lru_combine, (a, bterm), axis=1)
    y = hs.astype(h.dtype) * jax.nn.silu(g)
    return y @ w_out


def setup_inputs(seed: int = 0) -> dict:
    key = jax.random.key(seed)
    ks = jax.random.split(key, 24)
    f32 = jnp.float32
    nrm = lambda k, shape, scale: jax.random.normal(k, shape, f32) * scale
    u_lam = jax.random.uniform(ks[23], (N_B, D_BRANCH), f32, minval=0.9, maxval=0.999)
    a_base = u_lam ** (1.0 / LRU_C)
    lam = jnp.log(a_base) - jnp.log1p(-a_base)
    return {
        "x": nrm(ks[0], (BATCH, SEQ, D_MODEL), 1.0),
        "c": nrm(ks[1], (BATCH, D_MODEL), 1.0),
        "mod_w": nrm(ks[2], (DEPTH, D_MODEL, 3 * D_MODEL), D_MODEL ** -0.5),
        "mod_b": nrm(ks[3], (DEPTH, 3 * D_MODEL), 0.02),
        "pre_norm": 1.0 + nrm(ks[4], (DEPTH, D_MODEL), 0.1),
        "post_norm": 1.0 + nrm(ks[5], (DEPTH, D_MODEL), 0.1),
        "a_w_in": nrm(ks[6], (N_A, D_MODEL, 3 * D_BRANCH), D_MODEL ** -0.5),
        "a_v_norm": 1.0 + nrm(ks[7], (N_A, D_BRANCH), 0.1),
        "a_w_s": nrm(ks[8], (N_A, GMLP_GROUPS, CHUNK, CHUNK), CHUNK ** -0.5),
        "a_b_s": 1.0 + nrm(ks[9], (N_A, GMLP_GROUPS, CHUNK), 0.1),
        "a_w_out": nrm(ks[10], (N_A, D_BRANCH, D_MODEL), D_BRANCH ** -0.5),
        "b_w_in": nrm(ks[11], (N_B, D_MODEL, 2 * D_BRANCH), D_MODEL ** -0.5),
        "b_conv_w": nrm(ks[12], (N_B, CONV_WIDTH, D_BRANCH), CONV_WIDTH ** -0.5),
        "b_conv_b": nrm(ks[13], (N_B, D_BRANCH), 0.01),
        "b_ga_w": nrm(ks[14], (N_B, LRU_HEADS, LRU_HEAD_W, LRU_HEAD_W), LRU_HEAD_W ** -0.5),
        "b_ga_b": nrm(ks[15], (N_B, D_BRANCH), 0.01),
        "b_gx_w": nrm(ks[16], (N_B, LRU_HEADS, LRU_HEAD_W, LRU_HEAD_W), LRU_HEAD_W ** -0.5),
        "b_gx_b": nrm(ks[17], (N_B, D_BRANCH), 0.01),
        "b_lambda": lam,
        "b_w_out": nrm(ks[18], (N_B, D_BRANCH, D_MODEL), D_BRANCH ** -0.5),
    }


def reference(x, c, mod_w, mod_b, pre_norm, post_norm,
              a_w_in, a_v_norm, a_w_s, a_b_s, a_w_out,
              b_w_in, b_conv_w, b_conv_b, b_ga_w, b_ga_b, b_gx_w, b_gx_b, b_lambda, b_w_out):
    cond = jax.nn.silu(c)
    for layer in range(DEPTH):
        mod = cond @ mod_w[layer] + mod_b[layer]
        shift, scale, gate = jnp.split(mod[:, None, :], 3, axis=-1)
        h = rms_norm(x, pre_norm[layer]) * (1.0 + scale) + shift
        j = layer // N_MIXERS
        if layer % N_MIXERS == 0:
            y = gmlp_mixer(h, a_w_in[j], a_v_norm[j], a_w_s[j], a_b_s[j], a_w_out[j])
        else:
            y = rglru_mixer(h, b_w_in[j], b_conv_w[j], b_conv_b[j], b_ga_w[j], b_ga_b[j],
                            b_gx_w[j], b_gx_b[j], b_lambda[j], b_w_out[j])
        x = x + gate * rms_norm(y, post_norm[layer])
    return x
```

```python
import numpy as np
from contextlib import ExitStack
import concourse.bass as bass
import concourse.mybir as mybir
from concourse.bass_utils import run_bass_kernel_spmd

F32 = mybir.dt.float32
BF16 = mybir.dt.bfloat16
AF = mybir.ActivationFunctionType
ALU = mybir.AluOpType
ENGS = ["pe", "act", "dve", "pool", "sp"]

D = 1024
KC = 8
E = 2048
EC = 16
T = 512
TC = 4
SEQ = 4096
DEPTH = 4
EPS = 1e-6
NSLOT = 6
NPAR = 456

P_PRE, P_POST, P_MODB, P_VN, P_CW, P_CB, P_GAB, P_GXB, P_LAM, P_C = 0, 32, 64, 160, 192, 320, 352, 384, 416, 448


class Op:
    __slots__ = ("eng", "fn", "deps", "pos", "sig", "dma_key", "dma_val", "needed")


class Prog:
    def __init__(self, nc):
        self.nc = nc
        self.ops = {e: [] for e in ENGS}
        self.last_w = {}
        self.readers = {}
        self.dma_cnt = {}
        self.nops = 0

    def add(self, eng, fn, reads=(), writes=(), deps=(), dma_key=None):
        op = Op()
        op.eng = eng
        op.fn = fn
        op.needed = False
        op.sig = None
        op.dma_key = dma_key
        op.dma_val = None
        d = set(x for x in deps if x is not None)
        for r in reads:
            w = self.last_w.get(r)
            if w is not None:
                d.add(w)
        for r in writes:
            w = self.last_w.get(r)
            if w is not None:
                d.add(w)
            for rd in self.readers.get(r, ()):
                d.add(rd)
        if dma_key is not None:
            self.dma_cnt[dma_key] = self.dma_cnt.get(dma_key, 0) + 16
            op.dma_val = self.dma_cnt[dma_key]
        for r in reads:
            self.readers.setdefault(r, []).append(op)
        for r in writes:
            self.last_w[r] = op
            self.readers[r] = []
        d.discard(op)
        if eng == "pe":
            d = set(x for x in d if not (x.eng == "pe" and x.dma_key is None))
        latest = {}
        keep = set()
        for x in d:
            if x.dma_key is not None:
                keep.add(x)
            elif x.eng not in latest or latest[x.eng].pos < x.pos:
                latest[x.eng] = x
        d = keep | set(latest.values())
        op.deps = d
        for x in d:
            x.needed = True
        self.ops[eng].append(op)
        op.pos = self.nops
        self.nops += 1
        return op

    def emit(self, final_waits=()):
        nc = self.nc
        for x in final_waits:
            x.needed = True
        for e in ENGS:
            c = 0
            for op in self.ops[e]:
                if op.dma_key is None and op.needed:
                    c += 1
                    op.sig = c
        with ExitStack() as st:
            esem = {e: st.enter_context(nc.semaphore("s_" + e)) for e in ENGS}
            dsem = {k: st.enter_context(nc.semaphore("d_%d" % i))
                    for i, k in enumerate(self.dma_cnt.keys())}
            block = st.enter_context(nc.Block())

            def run(ename, h):
                waited = {}

                def do_wait(x):
                    if x.dma_key is not None:
                        s, v, key = dsem[x.dma_key], x.dma_val, ("d", x.dma_key)
                    else:
                        s, v, key = esem[x.eng], x.sig, ("e", x.eng)
                    if waited.get(key, 0) >= v:
                        return
                    waited[key] = v
                    h.wait_ge(s, v)

                for op in self.ops[ename]:
                    for x in sorted(op.deps, key=lambda o: o.pos):
                        do_wait(x)
                    ins = op.fn(h)
                    if op.dma_key is not None:
                        ins.then_inc(dsem[op.dma_key], 16)
                    elif op.needed:
                        ins.then_inc(esem[ename], 1)
                if ename == "sp":
                    for x in final_waits:
                        do_wait(x)

            @block.tensor
            def _(h):
                run("pe", h)

            @block.scalar
            def _(h):
                run("act", h)

            @block.vector
            def _(h):
                run("dve", h)

            @block.gpsimd
            def _(h):
                run("pool", h)

            @block.sync
            def _(h):
                run("sp", h)


def build(layers=(0, 1, 2, 3), ntiles=8):
    nc = bass.Bass("TRN2", target_bir_lowering=False)
    S = ntiles * T
    dr = lambda name, shape, kind="ExternalInput": nc.dram_tensor(name, shape, F32, kind=kind).ap()
    xT_d = dr("xT", [D, S])
    par_d = dr("par", [128, NPAR])
    bs_d = dr("bs", [1, 2 * 8 * 128])
    wsT_d = dr("wsT", [2, 8, 128, 128])
    mod_w = dr("mod_w", [DEPTH, D, 3 * D])
    a_w_in = dr("a_w_in", [2, D, 3 * E])
    a_w_out = dr("a_w_out", [2, E, D])
    b_w_in = dr("b_w_in", [2, D, 2 * E])
    b_ga_w = dr("b_ga_w", [2, 8, 256, 256])
    b_gx_w = dr("b_gx_w", [2, 8, 256, 256])
    b_w_out = dr("b_w_out", [2, E, D])
    oT_d = dr("oT", [D, S], kind="ExternalOutput")

    with ExitStack() as st:
        sb = lambda name, shape, dt: st.enter_context(nc.sbuf_tensor("s_" + name, shape, dt))
        P = Prog(nc)

        xT = [sb("xTs%d" % i, [128, KC, T], F32) for i in range(2)]
        hT = sb("hT", [128, KC, T], BF16)
        sq = sb("sq", [128, KC, T], BF16)
        oT = sb("oTs", [128, KC, T], F32)
        yT = sb("yT", [128, EC, T], BF16)
        ring = [sb("ring%d" % i, [128, 4096], BF16) for i in range(NSLOT)]
        NSC = 24
        scr = sb("scr", [128, NSC * 512], F32)
        ptab = sb("ptab", [128, NPAR], F32)
        bS = sb("bS", [128, 2048], F32)
        wsT = sb("wsT", [128, 16, 128], BF16)
        ones_bf = sb("ones_bf", [128, 128], BF16)
        one_f = sb("one_f", [1, 1], F32)
        mhalf1 = sb("mhalf", [128, 1], F32)
        phalf1 = sb("phalf", [128, 1], F32)

        def bc512(t1):
            a = t1[:]
            return bass.AP(a.tensor, a.offset, [list(a.ap[0]), [0, 512]])
        cond_bf = sb("cond_bf", [128, KC], BF16)
        modrow = sb("modrow", [1, 3 * D], F32)
        modT = sb("modT", [128, 24], F32)
        Acoef = sb("Acoef", [128, DEPTH, KC], F32)
        Bcoef = sb("Bcoef", [128, DEPTH, KC], F32)
        Gcoef = sb("Gcoef", [128, DEPTH, KC], F32)
        Kc = sb("Kc", [128, 2, EC], F32)
        gabh = sb("gabh", [128, 2, EC], F32)
        gxbh = sb("gxbh", [128, 2, EC], F32)
        lrutmp = sb("lrutmp", [128, 2, EC], F32)
        ctail = sb("ctail", [128, 2, EC, 4], F32)
        hstate = sb("hstate", [128, 2, EC], F32)
        rbc = sb("rbc", [128, T], F32)
        stats = sb("stats", [128, 2, 4, 6], F32)
        mv = sb("mv", [128, 2, 2], F32)
        rs1 = sb("rs1", [128, 2, 2], F32)
        PS = [st.enter_context(nc.psum_tensor("ps%d" % i, [128, 512], F32)) for i in range(8)]

        def sc(i, n=1):
            return scr[:, i * 512:(i + n) * 512]

        def sck(i, n=1):
            r = []
            for k in range(i, i + n):
                r += ["sc%da" % k, "sc%db" % k]
            return r

        bank_ctr = [0]

        def nb():
            b = bank_ctr[0] % 8
            bank_ctr[0] += 1
            return b

        A = P.add

        stream = []

        def full_chunk(src3):
            k, n = src3.shape[1], src3.shape[2]
            return [(lambda s, k=k, n=n: ring[s][:, 0:k * n].rearrange("p (k n) -> p k n", k=k), src3)]

        def win_chunk(w, col0, ncol):
            return full_chunk(w[:, col0:col0 + ncol].rearrange("(k p) n -> p k n", p=128))

        def wout_chunk(w, col0):
            return full_chunk(w[:, col0:col0 + 256].rearrange("(k p) n -> p k n", p=128))

        def gate_chunk(lb, q):
            specs = []
            for gi, gw in enumerate((b_ga_w, b_gx_w)):
                for h2 in range(2):
                    src = gw[lb, 2 * q + h2].rearrange("(i p) n -> p i n", p=128)
                    off = ((h2 * 2 + gi) * 2) * 256
                    specs.append((lambda s, off=off: ring[s][:, off:off + 512].rearrange("p (i n) -> p i n", i=2), src))
            return specs

        for ti in range(ntiles):
            for l in layers:
                j = l // 2
                if ti == 0:
                    for fc in range(6):
                        stream.append(win_chunk(mod_w[l], fc * 512, 512))
                if l % 2 == 0:
                    for fc in range(4):
                        stream.append(win_chunk(a_w_in[j], E + fc * 512, 512))
                    for q in range(4):
                        stream.append(win_chunk(a_w_in[j], q * 512, 512))
                        stream.append(win_chunk(a_w_in[j], 2 * E + q * 512, 512))
                    for q in range(4):
                        stream.append(wout_chunk(a_w_out[j], q * 256))
                else:
                    for q in range(4):
                        stream.append(win_chunk(b_w_in[j], q * 512, 512))
                        stream.append(win_chunk(b_w_in[j], E + q * 512, 512))
                        stream.append(gate_chunk(j, q))
                    for q in range(4):
                        stream.append(wout_chunk(b_w_out[j], q * 256))

        rstate = {"next_load": 0, "next_use": 0}

        def slotk(s):
            return ["slot%d_%d" % (s, k) for k in range(4)]

        def ring_load(ci):
            s = ci % NSLOT
            specs = stream[ci]
            for k, (dst_fn, src) in enumerate(specs):
                wk = slotk(s) if len(specs) == 1 else ["slot%d_%d" % (s, k)]
                A("pool", lambda h, dst_fn=dst_fn, src=src, s=s: h.dma_start(out=dst_fn(s), in_=src),
                  writes=wk, dma_key="slot%d" % s)

        def ring_use():
            ci = rstate["next_use"]
            rstate["next_use"] += 1
            assert ci < rstate["next_load"], "ring underflow"
            return ci, ci % NSLOT

        def ring_done(ci):
            nl = rstate["next_load"]
            assert nl == ci + NSLOT or nl >= len(stream), (nl, ci)
            if nl < len(stream):
                ring_load(nl)
                rstate["next_load"] += 1

        A("sp", lambda h: h.dma_start(out=ptab[:], in_=par_d[:]), writes=["ptab"], dma_key="ptab")
        A("sp", lambda h: h.dma_start(out=bS[:], in_=bs_d.partition_broadcast(128)), writes=["bS"], dma_key="bS")
        wsraw = scr[:, 0:2048].rearrange("p (a b) -> p a b", a=16)
        A("sp", lambda h: h.dma_start(out=wsraw, in_=wsT_d.rearrange("j g s t -> s (j g) t")), writes=sck(0, 4), dma_key="wsraw")
        for ci in range(min(NSLOT, len(stream))):
            ring_load(ci)
            rstate["next_load"] += 1
        A("dve", lambda h: h.memset(ones_bf[:], 1.0), writes=["ones_bf"])
        A("dve", lambda h: h.memset(one_f[:], 1.0), writes=["one_f"])
        A("dve", lambda h: h.memset(mhalf1[:], -0.5), writes=["mhalf"])
        A("dve", lambda h: h.memset(phalf1[:], 0.5), writes=["phalf"])
        A("dve", lambda h: h.memset(ctail[:], 0.0), writes=["ctail"])
        A("dve", lambda h: h.memset(hstate[:], 0.0), writes=["hstate"])
        A("pool", lambda h: h.affine_select(out=wsT[:], in_=wsraw, pattern=[[0, 16], [1, 128]],
                                            compare_op=ALU.is_ge, fill=0.0, base=0, channel_multiplier=-1),
          reads=sck(0, 4), writes=["wsT"])
        A("act", lambda h: h.activation(out=cond_bf[:], in_=ptab[:, P_C:P_C + 8], func=AF.Silu), reads=["ptab"], writes=["cond"])
        lam_v = ptab[:, P_LAM:P_LAM + 32].rearrange("p (a b) -> p a b", a=2)
        A("act", lambda h: h.activation(out=lrutmp[:], in_=lam_v, func=AF.Exp, scale=-1.0), reads=["ptab"], writes=["lrutmp"])
        A("act", lambda h: h.activation(out=Kc[:], in_=lrutmp[:], func=AF.Ln, bias=1.0, scale=1.0), reads=["lrutmp"], writes=["Kc"])
        A("dve", lambda h: h.tensor_scalar(out=Kc[:], in0=Kc[:], scalar1=-8.0, scalar2=None, op0=ALU.mult), reads=["Kc"], writes=["Kc"])
        A("dve", lambda h: h.tensor_scalar(out=gabh[:], in0=ptab[:, P_GAB:P_GAB + 32].rearrange("p (a b) -> p a b", a=2),
                                           scalar1=0.5, scalar2=None, op0=ALU.mult), reads=["ptab"], writes=["gabh"])
        A("dve", lambda h: h.tensor_scalar(out=gxbh[:], in0=ptab[:, P_GXB:P_GXB + 32].rearrange("p (a b) -> p a b", a=2),
                                           scalar1=0.5, scalar2=None, op0=ALU.mult), reads=["ptab"], writes=["gxbh"])

        def rms_bcast(src_sq_key, dst, dst_key):
            b = nb()
            for kc in range(KC):
                A("pe", lambda h, kc=kc, b=b: h.matmul(PS[b][:], lhsT=ones_bf[:], rhs=sq[:, kc, :], start=(kc == 0), stop=(kc == KC - 1)),
                  reads=["ones_bf", src_sq_key + str(kc)], writes=["ps%d" % b])
            A("dve", lambda h, b=b: h.tensor_scalar(out=dst, in0=PS[b][:], scalar1=1.0 / D, scalar2=EPS, op0=ALU.mult, op1=ALU.add),
              reads=["ps%d" % b], writes=[dst_key])
            A("pool", lambda h: h.tensor_tensor(out=dst, in0=dst, in1=bc512(mhalf1), op=ALU.pow), reads=[dst_key, "mhalf"], writes=[dst_key])

        def mod_compute(l):
            for fc in range(6):
                ci, s = ring_use()
                b = nb()
                for kc in range(KC):
                    A("pe", lambda h, kc=kc, b=b, s=s: h.matmul(PS[b][0:1, :], lhsT=cond_bf[:, kc:kc + 1], rhs=ring[s][:, kc * 512:(kc + 1) * 512],
                                                                start=(kc == 0), stop=(kc == KC - 1)),
                      reads=["cond", ] + slotk(s), writes=["ps%d" % b])
                A("act", lambda h, b=b, fc=fc: h.activation(out=modrow[0:1, fc * 512:(fc + 1) * 512], in_=PS[b][0:1, :], func=AF.Copy),
                  reads=["ps%d" % b], writes=["modrow"])
                ring_done(ci)
            b = nb()
            for jj in range(24):
                A("pe", lambda h, jj=jj, b=b: h.matmul(PS[b][:, jj:jj + 1], lhsT=modrow[0:1, jj * 128:(jj + 1) * 128], rhs=one_f[0:1, 0:1],
                                                       start=True, stop=True),
                  reads=["modrow", "one_f"], writes=["ps%d" % b])
            A("dve", lambda h, b=b: h.tensor_tensor(out=modT[:], in0=PS[b][:, 0:24], in1=ptab[:, P_MODB + l * 24:P_MODB + (l + 1) * 24], op=ALU.add),
              reads=["ps%d" % b, "ptab"], writes=["modT"])
            A("dve", lambda h: h.scalar_tensor_tensor(out=Acoef[:, l, :], in0=modT[:, 8:16], scalar=1.0, in1=ptab[:, P_PRE + l * 8:P_PRE + (l + 1) * 8],
                                                      op0=ALU.add, op1=ALU.mult), reads=["modT", "ptab"], writes=["coef%d" % l])
            A("dve", lambda h: h.tensor_copy(out=Bcoef[:, l, :], in_=modT[:, 0:8]), reads=["modT"], writes=["coef%d" % l])
            A("dve", lambda h: h.tensor_tensor(out=Gcoef[:, l, :], in0=modT[:, 16:24], in1=ptab[:, P_POST + l * 8:P_POST + (l + 1) * 8], op=ALU.mult),
              reads=["modT", "ptab"], writes=["coef%d" % l])

        def pre_norm(l, X, xk):
            for kc in range(KC):
                A("act", lambda h, kc=kc: h.activation(out=sq[:, kc, :], in_=X[:, kc, :], func=AF.Square), reads=[xk + str(kc)], writes=["sq%d" % kc])
            rms_bcast("sq", rbc[:], "rbc")
            for kc in range(KC):
                t = sc(20 + kc % 2)
                tk = sck(20 + kc % 2)
                A("dve", lambda h, kc=kc, t=t: h.tensor_tensor(out=t, in0=X[:, kc, :], in1=rbc[:], op=ALU.mult), reads=[xk + str(kc), "rbc"], writes=tk)
                A("act", lambda h, kc=kc, t=t: h.activation(out=hT[:, kc, :], in_=t, func=AF.Identity, scale=Acoef[:, l, kc:kc + 1], bias=Bcoef[:, l, kc:kc + 1]),
                  reads=tk + ["coef%d" % l], writes=["hT%d" % kc])

        def out_proj_post(l, X, xk, slots_fn):
            ci = s = None
            for fo in range(KC):
                if fo % 2 == 0:
                    ci, s = ring_use()
                b = nb()
                for kc in range(EC):
                    A("pe", lambda h, kc=kc, b=b, s=s, fo=fo: h.matmul(PS[b][:], lhsT=ring[s][:, kc * 256 + (fo % 2) * 128:kc * 256 + (fo % 2) * 128 + 128],
                                                                       rhs=yT[:, kc, :], start=(kc == 0), stop=(kc == EC - 1)),
                      reads=slotk(s) + ["yT%d" % kc], writes=["ps%d" % b])
                A("act", lambda h, b=b, fo=fo: h.activation(out=oT[:, fo, :], in_=PS[b][:], func=AF.Copy), reads=["ps%d" % b], writes=["oT%d" % fo])
                A("act", lambda h, b=b, fo=fo: h.activation(out=sq[:, fo, :], in_=PS[b][:], func=AF.Square), reads=["ps%d" % b], writes=["sq%d" % fo])
                if fo % 2 == 1:
                    ring_done(ci)
            rms_bcast("sq", rbc[:], "rbc")
            for fo in range(KC):
                t = sc(20 + fo % 2)
                tk = sck(20 + fo % 2)
                A("dve", lambda h, fo=fo, t=t: h.tensor_tensor(out=t, in0=oT[:, fo, :], in1=rbc[:], op=ALU.mult), reads=["oT%d" % fo, "rbc"], writes=tk)
                A("dve", lambda h, fo=fo, t=t: h.scalar_tensor_tensor(out=X[:, fo, :], in0=t, scalar=Gcoef[:, l, fo:fo + 1], in1=X[:, fo, :],
                                                                      op0=ALU.mult, op1=ALU.add),
                  reads=tk + ["coef%d" % l, xk + str(fo)], writes=[xk + str(fo)])

        def layer_a(l, X, xk):
            j = l // 2
            pre_norm(l, X, xk)
            vslots = [ring_use() for _ in range(4)]
            vn = scr[:, 8 * 512:16 * 512].bitcast(BF16)
            for tc in range(TC):
                gi = tc % 2
                gv = sc(gi * 4, 4)
                for fc in range(4):
                    ci, s = vslots[fc]
                    b = nb()
                    for kc in range(KC):
                        A("pe", lambda h, kc=kc, b=b, s=s, tc=tc: h.matmul(PS[b][:], lhsT=hT[:, kc, tc * 128:(tc + 1) * 128], rhs=ring[s][:, kc * 512:(kc + 1) * 512],
                                                                           start=(kc == 0), stop=(kc == KC - 1)),
                          reads=["hT%d" % kc, ] + slotk(s), writes=["ps%d" % b])
                    A("act", lambda h, b=b, fc=fc, gv=gv: h.activation(out=gv[:, fc * 512:(fc + 1) * 512], in_=PS[b][:], func=AF.Gelu_apprx_tanh),
                      reads=["ps%d" % b], writes=sck(gi * 4 + fc))
                    A("dve", lambda h, fc=fc, gv=gv, gi=gi: h.bn_stats(out=stats[:, gi, fc, :], in_=gv[:, fc * 512:(fc + 1) * 512]),
                      reads=sck(gi * 4 + fc), writes=["stats%d" % gi])
                A("dve", lambda h, gi=gi: h.bn_aggr(out=mv[:, gi, :], in_=stats[:, gi, :, :].rearrange("p a b -> p (a b)")), reads=["stats%d" % gi], writes=["mv%d" % gi])
                A("dve", lambda h, gi=gi: h.tensor_scalar(out=rs1[:, gi, 0:1], in0=mv[:, gi, 1:2], scalar1=EPS, scalar2=None, op0=ALU.add),
                  reads=["mv%d" % gi], writes=["rs%d" % gi])
                A("pool", lambda h, gi=gi: h.tensor_tensor(out=rs1[:, gi, 1:2], in0=rs1[:, gi, 0:1], in1=mhalf1[:], op=ALU.pow),
                  reads=["rs%d" % gi, "mhalf"], writes=["rs%d" % gi])
                A("dve", lambda h, gi=gi, gv=gv, tc=tc: h.tensor_scalar(out=vn[:, tc * 2048:(tc + 1) * 2048], in0=gv, scalar1=mv[:, gi, 0:1], scalar2=rs1[:, gi, 1:2],
                                                                        op0=ALU.subtract, op1=ALU.mult),
                  reads=sck(gi * 4, 4) + ["mv%d" % gi, "rs%d" % gi], writes=sck(8 + tc * 2, 2))
            for (ci, s) in vslots:
                ring_done(ci)

            def mix(cc):
                g = cc // 2
                b = nb()
                for tc in range(TC):
                    A("pe", lambda h, tc=tc, b=b: h.matmul(PS[b][:, tc * 128:(tc + 1) * 128], lhsT=vn[:, tc * 2048 + cc * 128:tc * 2048 + (cc + 1) * 128],
                                                           rhs=wsT[:, j * 8 + g, :], start=True, stop=True),
                      reads=sck(8 + tc * 2, 2) + ["wsT"], writes=["ps%d" % b])
                return b

            pend = None
            uci = gci = us = gs = None
            for cc in range(EC + 1):
                if cc < EC:
                    if cc % 4 == 0:
                        uci, us = ring_use()
                        gci, gs = ring_use()
                    off = (cc % 4) * 128
                    bu = nb()
                    for kc in range(KC):
                        A("pe", lambda h, kc=kc, b=bu, s=us, off=off: h.matmul(PS[b][:], lhsT=ring[s][:, kc * 512 + off:kc * 512 + off + 128], rhs=hT[:, kc, :],
                                                                               start=(kc == 0), stop=(kc == KC - 1)),
                          reads=[] + slotk(us) + ["hT%d" % kc], writes=["ps%d" % bu])
                    bg = nb()
                    for kc in range(KC):
                        A("pe", lambda h, kc=kc, b=bg, s=gs, off=off: h.matmul(PS[b][:], lhsT=ring[s][:, kc * 512 + off:kc * 512 + off + 128], rhs=hT[:, kc, :],
                                                                               start=(kc == 0), stop=(kc == KC - 1)),
                          reads=[] + slotk(gs) + ["hT%d" % kc], writes=["ps%d" % bg])
                    if cc % 4 == 3:
                        ring_done(uci)
                        ring_done(gci)
                    par = cc % 2
                    t_gu, t_th, t_sg, t_m = 16 + par * 2, 17 + par * 2, 0 + par * 2, 1 + par * 2
                    A("act", lambda h, b=bu, t=t_gu: h.activation(out=sc(t), in_=PS[b][:], func=AF.Gelu_apprx_tanh), reads=["ps%d" % bu], writes=sck(t_gu))
                    A("act", lambda h, b=bg, t=t_th: h.activation(out=sc(t), in_=PS[b][:], func=AF.Tanh, scale=0.5), reads=["ps%d" % bg], writes=sck(t_th))
                    A("dve", lambda h, b=bg, t=t_th, o=t_sg: h.scalar_tensor_tensor(out=sc(o), in0=sc(t), scalar=1.0, in1=PS[b][:], op0=ALU.add, op1=ALU.mult),
                      reads=sck(t_th) + ["ps%d" % bg], writes=sck(t_sg))
                    A("dve", lambda h, a=t_gu, o=t_sg: h.tensor_tensor(out=sc(o), in0=sc(a), in1=sc(o), op=ALU.mult), reads=sck(t_gu) + sck(t_sg), writes=sck(t_sg))
                    cur = (cc, t_sg, t_m)
                else:
                    cur = None
                if pend is not None:
                    pcc, p_sg, p_m = pend
                    bm = mix(pcc)
                    g = pcc // 2
                    bs_ap = bass.AP(bS[:].tensor, bS[:, (j * 8 + g) * 128:(j * 8 + g) * 128 + 128].offset, [list(bS[:].ap[0]), [0, 4], [1, 128]])
                    A("dve", lambda h, b=bm, o=p_m, pcc=pcc, bs_ap=bs_ap: h.scalar_tensor_tensor(
                        out=sc(o).rearrange("p (a b) -> p a b", a=4), in0=PS[b][:].rearrange("p (a b) -> p a b", a=4),
                        scalar=ptab[:, P_VN + j * 16 + pcc:P_VN + j * 16 + pcc + 1], in1=bs_ap, op0=ALU.mult, op1=ALU.add),
                      reads=["ps%d" % bm, "ptab", "bS"], writes=sck(p_m))
                    A("dve", lambda h, o=p_m, s_=p_sg, pcc=pcc: h.scalar_tensor_tensor(out=yT[:, pcc, :], in0=sc(o), scalar=0.5, in1=sc(s_), op0=ALU.mult, op1=ALU.mult),
                      reads=sck(p_m) + sck(p_sg), writes=["yT%d" % pcc])
                pend = cur
            out_proj_post(l, X, xk, None)

        def layer_b(l, X, xk):
            lb = l // 2
            pre_norm(l, X, xk)
            def xbuf(hp, ci, a=0, n=515):
                o = (hp * 2 + ci) * 1024
                return scr[:, o + a:o + a + n]
            xbk = lambda hp, ci: sck((hp * 2 + ci) * 2, 2)
            xb_all = xbk
            XC = 8
            XCB = 12
            xcb_all = scr[:, XCB * 512:(XCB + 2) * 512].bitcast(BF16)
            xcbk = lambda hp, ci: ["sc%d%s" % (XCB + (hp * 2 + ci) // 2, "ab"[(hp * 2 + ci) % 2])]
            TH = 14
            state = {"bank_x": {}, "bank_g": {}}
            xq = {}

            def pe_xb(hd):
                hp = hd % 2
                if hd % 2 == 0:
                    xq["x"] = ring_use()
                ci_, s = xq["x"]
                for ci in range(2):
                    b = nb()
                    off = ((hd % 2) * 2 + ci) * 128
                    for kc in range(KC):
                        A("pe", lambda h, kc=kc, b=b, s=s, off=off: h.matmul(PS[b][:], lhsT=ring[s][:, kc * 512 + off:kc * 512 + off + 128], rhs=hT[:, kc, :],
                                                                             start=(kc == 0), stop=(kc == KC - 1)),
                          reads=slotk(s) + ["hT%d" % kc], writes=["ps%d" % b])
                    cc = hd * 2 + ci
                    A("dve", lambda h, hp=hp, ci=ci, cc=cc: h.tensor_copy(out=xbuf(hp, ci, 0, 3), in_=ctail[:, lb, cc, 0:3]),
                      reads=["ctail%d_%d" % (lb, cc)], writes=xb_all(hp, ci))
                    A("act", lambda h, b=b, hp=hp, ci=ci: h.activation(out=xbuf(hp, ci, 3, 512), in_=PS[b][:], func=AF.Copy),
                      reads=["ps%d" % b], writes=xb_all(hp, ci))
                    A("dve", lambda h, hp=hp, ci=ci, cc=cc: h.tensor_copy(out=ctail[:, lb, cc, 0:3], in_=xbuf(hp, ci, 512, 3)),
                      reads=xbk(hp, ci), writes=["ctail%d_%d" % (lb, cc)])
                    xc = sc(XC + hp * 2 + ci)
                    xck = sck(XC + hp * 2 + ci)
                    cw = lambda k, cc=cc: ptab[:, P_CW + (lb * 4 + k) * 16 + cc:P_CW + (lb * 4 + k) * 16 + cc + 1]
                    A("act", lambda h, b=b, xc=xc, cc=cc, cw=cw: h.activation(out=xc, in_=PS[b][:], func=AF.Identity, scale=cw(3),
                                                                             bias=ptab[:, P_CB + lb * 16 + cc:P_CB + lb * 16 + cc + 1]),
                      reads=["ps%d" % b, "ptab"], writes=xck)
                    for k in (2, 1, 0):
                        A("dve", lambda h, k=k, xc=xc, hp=hp, ci=ci, cw=cw: h.scalar_tensor_tensor(out=xc, in0=xbuf(hp, ci, k, 512), scalar=cw(k), in1=xc,
                                                                                                   op0=ALU.mult, op1=ALU.add),
                          reads=xbk(hp, ci) + xck + ["ptab"], writes=xck)
                    A("act", lambda h, xc=xc, hp=hp, ci=ci: h.activation(out=xcb_all[:, (hp * 2 + ci) * 512:(hp * 2 + ci + 1) * 512], in_=xc, func=AF.Copy),
                      reads=xck, writes=xcbk(hp, ci))
                if hd % 2 == 1:
                    ring_done(ci_)

            def pe_g(hd):
                if hd % 2 == 0:
                    xq["g"] = ring_use()
                ci_, s = xq["g"]
                banks = []
                for ci in range(2):
                    b = nb()
                    off = ((hd % 2) * 2 + ci) * 128
                    for kc in range(KC):
                        A("pe", lambda h, kc=kc, b=b, s=s, off=off: h.matmul(PS[b][:], lhsT=ring[s][:, kc * 512 + off:kc * 512 + off + 128], rhs=hT[:, kc, :],
                                                                             start=(kc == 0), stop=(kc == KC - 1)),
                          reads=slotk(s) + ["hT%d" % kc], writes=["ps%d" % b])
                    t = 20 + (hd % 2) * 2 + ci
                    A("act", lambda h, b=b, t=t: h.activation(out=sc(t), in_=PS[b][:], func=AF.Tanh, scale=0.5), reads=["ps%d" % b], writes=sck(t))
                    A("dve", lambda h, b=b, t=t: h.scalar_tensor_tensor(out=sc(t), in0=sc(t), scalar=1.0, in1=PS[b][:], op0=ALU.add, op1=ALU.mult),
                      reads=sck(t) + ["ps%d" % b], writes=sck(t))
                if hd % 2 == 1:
                    ring_done(ci_)

            def pe_gates(hd):
                hp = hd % 2
                if hd % 2 == 0:
                    xq["w"] = ring_use()
                ci_, s = xq["w"]
                h2 = hd % 2
                for jc in range(2):
                    cc = hd * 2 + jc
                    br = nb()
                    bi = nb()
                    for gi, b in ((0, br), (1, bi)):
                        for ic in range(2):
                            woff = ((h2 * 2 + gi) * 2 + ic) * 256 + jc * 128
                            A("pe", lambda h, b=b, s=s, woff=woff, ic=ic, hp=hp: h.matmul(PS[b][:], lhsT=ring[s][:, woff:woff + 128],
                                                                                         rhs=xcb_all[:, (hp * 2 + ic) * 512:(hp * 2 + ic + 1) * 512],
                                                                                         start=(ic == 0), stop=(ic == 1)),
                              reads=slotk(s) + xcbk(hp, ic), writes=["ps%d" % b])
                    t_r, t_i, t_a2, t_om = TH + jc * 3, TH + jc * 3 + 1, TH + jc * 3 + 2, TH + jc * 3 + 1
                    kcol = Kc[:, lb, cc:cc + 1]
                    A("act", lambda h, b=br, t=t_r, cc=cc: h.activation(out=sc(t), in_=PS[b][:], func=AF.Tanh, scale=0.5, bias=gabh[:, lb, cc:cc + 1]),
                      reads=["ps%d" % br, "gabh"], writes=sck(t_r))
                    A("act", lambda h, b=bi, t=t_i, cc=cc: h.activation(out=sc(t), in_=PS[b][:], func=AF.Tanh, scale=0.5, bias=gxbh[:, lb, cc:cc + 1]),
                      reads=["ps%d" % bi, "gxbh"], writes=sck(t_i))
                    A("act", lambda h, t=t_r, o=t_a2, kcol=kcol: h.activation(out=sc(o), in_=sc(t), func=AF.Exp, scale=kcol, bias=kcol),
                      reads=sck(t_r) + ["Kc"], writes=sck(t_a2))
                    xc = sc(XC + hp * 2 + jc)
                    xck = sck(XC + hp * 2 + jc)
                    A("dve", lambda h, t=t_i, xc=xc: h.scalar_tensor_tensor(out=sc(t), in0=sc(t), scalar=1.0, in1=xc, op0=ALU.add, op1=ALU.mult),
                      reads=sck(t_i) + xck, writes=sck(t_i))
                    A("dve", lambda h, o=t_a2, r=t_r: h.tensor_scalar(out=sc(r), in0=sc(o), scalar1=-1.0, scalar2=1.0, op0=ALU.mult, op1=ALU.add),
                      reads=sck(t_a2), writes=sck(t_r))
                    A("pool", lambda h, r=t_r: h.tensor_tensor(out=sc(r), in0=sc(r), in1=bc512(phalf1), op=ALU.pow), reads=sck(t_r) + ["phalf"], writes=sck(t_r))
                    A("pool", lambda h, o=t_a2: h.tensor_tensor(out=sc(o), in0=sc(o), in1=bc512(phalf1), op=ALU.pow), reads=sck(t_a2) + ["phalf"], writes=sck(t_a2))
                    A("dve", lambda h, t=t_i, r=t_r: h.scalar_tensor_tensor(out=sc(t), in0=sc(t), scalar=0.5, in1=sc(r), op0=ALU.mult, op1=ALU.mult),
                      reads=sck(t_i) + sck(t_r), writes=sck(t_i))
                    A("dve", lambda h, o=t_a2, t=t_i, r=t_r, cc=cc: h.tensor_tensor_scan(out=sc(r), data0=sc(o), data1=sc(t), initial=hstate[:, lb, cc:cc + 1],
                                                                                        op0=ALU.mult, op1=ALU.add),
                      reads=sck(t_a2) + sck(t_i) + ["hst%d_%d" % (lb, cc)], writes=sck(t_r))
                    A("dve", lambda h, r=t_r, cc=cc: h.tensor_copy(out=hstate[:, lb, cc:cc + 1], in_=sc(r)[:, 511:512]), reads=sck(t_r), writes=["hst%d_%d" % (lb, cc)])
                    tg = 20 + (hd % 2) * 2 + jc
                    A("dve", lambda h, r=t_r, tg=tg, cc=cc: h.scalar_tensor_tensor(out=yT[:, cc, :], in0=sc(r), scalar=0.5, in1=sc(tg), op0=ALU.mult, op1=ALU.mult),
                      reads=sck(t_r) + sck(tg), writes=["yT%d" % cc])
                if hd % 2 == 1:
                    ring_done(ci_)

            pe_xb(0)
            pe_xb(1)
            for hd in range(8):
                pe_g(hd)
                pe_gates(hd)
                if hd + 2 < 8:
                    pe_xb(hd + 2)
            out_proj_post(l, X, xk, None)

        x_loads = {}

        def load_x(ti):
            X = xT[ti % 2]
            xk = "x%d_" % (ti % 2)
            x_loads[ti] = A("sp", lambda h, ti=ti, X=X: h.dma_start(out=X[:], in_=xT_d[:, ti * T:(ti + 1) * T].rearrange("(k p) t -> p k t", p=128)),
                            writes=[xk + str(kc) for kc in range(KC)], dma_key="xin%d" % (ti % 2))

        out_ops = []
        load_x(0)
        for ti in range(ntiles):
            X = xT[ti % 2]
            xk = "x%d_" % (ti % 2)
            if ti + 1 < ntiles:
                load_x(ti + 1)
            for l in layers:
                if ti == 0:
                    mod_compute(l)
                if l % 2 == 0:
                    layer_a(l, X, xk)
                else:
                    layer_b(l, X, xk)
            out_ops.append(A("sp", lambda h, ti=ti, X=X: h.dma_start(out=oT_d[:, ti * T:(ti + 1) * T].rearrange("(k p) t -> p k t", p=128), in_=X[:]),
                             reads=[xk + str(kc) for kc in range(KC)], dma_key="xout%d" % (ti % 2)))
        assert rstate["next_use"] == len(stream), (rstate, len(stream))
        P.emit(final_waits=out_ops)
    return nc


_CACHE = {}


def _get_nc(layers, ntiles):
    key = (tuple(layers), ntiles)
    if key not in _CACHE:
        _CACHE[key] = build(layers, ntiles)
    return _CACHE[key]


def make_in_maps(inp, batches, ntiles, x_override=None):
    f = lambda a: np.ascontiguousarray(np.asarray(a, dtype=np.float32))
    rows = [f(inp["pre_norm"]).reshape(32, 128), f(inp["post_norm"]).reshape(32, 128), f(inp["mod_b"]).reshape(96, 128),
            f(inp["a_v_norm"]).reshape(32, 128), f(inp["b_conv_w"]).reshape(128, 128), f(inp["b_conv_b"]).reshape(32, 128),
            f(inp["b_ga_b"]).reshape(32, 128), f(inp["b_gx_b"]).reshape(32, 128), f(inp["b_lambda"]).reshape(32, 128)]
    shared = {
        "bs": f(inp["a_b_s"]).reshape(1, 2048),
        "wsT": np.ascontiguousarray(np.transpose(f(inp["a_w_s"]), (0, 1, 3, 2))),
        "mod_w": f(inp["mod_w"]), "a_w_in": f(inp["a_w_in"]), "a_w_out": f(inp["a_w_out"]),
        "b_w_in": f(inp["b_w_in"]), "b_ga_w": f(inp["b_ga_w"]), "b_gx_w": f(inp["b_gx_w"]), "b_w_out": f(inp["b_w_out"]),
    }
    S = ntiles * T
    maps = []
    for bi, b in enumerate(batches):
        par = np.concatenate(rows + [f(inp["c"])[b].reshape(8, 128)], axis=0)
        xb = f(inp["x"])[b] if x_override is None else x_override[bi]
        m = dict(shared)
        m["par"] = np.ascontiguousarray(par.T)
        m["xT"] = np.ascontiguousarray(xb[:S].T)
        maps.append(m)
    return maps


def kernel(**inputs):
    nb_ = 8
    nc = _get_nc((0, 1, 2, 3), SEQ // T)
    maps = make_in_maps(inputs, list(range(nb_)), SEQ // T)
    res = run_bass_kernel_spmd(nc, maps, core_ids=list(range(nb_)))
    out = np.stack([np.ascontiguousarray(r["oT"].T) for r in res.results], axis=0)
    return out.astype(np.float32)
```

```python
import numpy as np
from contextlib import ExitStack
import concourse.bass as bass
import concourse.mybir as mybir
from concourse.bass_utils import run_bass_kernel_spmd

F32 = mybir.dt.float32
BF16 = mybir.dt.bfloat16
AF = mybir.ActivationFunctionType
ALU = mybir.AluOpType
ENGS = ["pe", "act", "dve", "pool", "sp"]

D = 1024
KC = 8
E = 2048
EC = 16
T = 512
TC = 4
SEQ = 4096
DEPTH = 4
EPS = 1e-6
NSLOT = 6
NPAR = 456

P_PRE, P_POST, P_MODB, P_VN, P_CW, P_CB, P_GAB, P_GXB, P_LAM, P_C = 0, 32, 64, 160, 192, 320, 352, 384, 416, 448


class Op:
    __slots__ = ("eng", "fn", "deps", "pos", "sig", "dma_key", "dma_val", "needed")


class Prog:
    def __init__(self, nc):
        self.nc = nc
        self.ops = {e: [] for e in ENGS}
        self.last_w = {}
        self.readers = {}
        self.dma_cnt = {}
        self.nops = 0

    def add(self, eng, fn, reads=(), writes=(), deps=(), dma_key=None):
        op = Op()
        op.eng = eng
        op.fn = fn
        op.needed = False
        op.sig = None
        op.dma_key = dma_key
        op.dma_val = None
        d = set(x for x in deps if x is not None)
        for r in reads:
            w = self.last_w.get(r)
            if w is not None:
                d.add(w)
        for r in writes:
            w = self.last_w.get(r)
            if w is not None:
                d.add(w)
            for rd in self.readers.get(r, ()):
                d.add(rd)
        if dma_key is not None:
            self.dma_cnt[dma_key] = self.dma_cnt.get(dma_key, 0) + 16
            op.dma_val = self.dma_cnt[dma_key]
        for r in reads:
            self.readers.setdefault(r, []).append(op)
        for r in writes:
            self.last_w[r] = op
            self.readers[r] = []
        d.discard(op)
        if eng == "pe":
            d = set(x for x in d if not (x.eng == "pe" and x.dma_key is None))
        latest = {}
        keep = set()
        for x in d:
            if x.dma_key is not None:
                keep.add(x)
            elif x.eng not in latest or latest[x.eng].pos < x.pos:
                latest[x.eng] = x
        d = keep | set(latest.values())
        op.deps = d
        for x in d:
            x.needed = True
        self.ops[eng].append(op)
        op.pos = self.nops
        self.nops += 1
        return op

    def emit(self, final_waits=()):
        nc = self.nc
        for x in final_waits:
            x.needed = True
        for e in ENGS:
            c = 0
            for op in self.ops[e]:
                if op.dma_key is None and op.needed:
                    c += 1
                    op.sig = c
        with ExitStack() as st:
            esem = {e: st.enter_context(nc.semaphore("s_" + e)) for e in ENGS}
            dsem = {k: st.enter_context(nc.semaphore("d_%d" % i))
                    for i, k in enumerate(self.dma_cnt.keys())}
            block = st.enter_context(nc.Block())

            def run(ename, h):
                waited = {}

                def do_wait(x):
                    if x.dma_key is not None:
                        s, v, key = dsem[x.dma_key], x.dma_val, ("d", x.dma_key)
                    else:
                        s, v, key = esem[x.eng], x.sig, ("e", x.eng)
                    if waited.get(key, 0) >= v:
                        return
                    waited[key] = v
                    h.wait_ge(s, v)

                for op in self.ops[ename]:
                    for x in sorted(op.deps, key=lambda o: o.pos):
                        do_wait(x)
                    ins = op.fn(h)
                    if op.dma_key is not None:
                        ins.then_inc(dsem[op.dma_key], 16)
                    elif op.needed:
                        ins.then_inc(esem[ename], 1)
                if ename == "sp":
                    for x in final_waits:
                        do_wait(x)

            @block.tensor
            def _(h):
                run("pe", h)

            @block.scalar
            def _(h):
                run("act", h)

            @block.vector
            def _(h):
                run("dve", h)

            @block.gpsimd
            def _(h):
                run("pool", h)

            @block.sync
            def _(h):
                run("sp", h)


def build(layers=(0, 1, 2, 3), ntiles=8):
    nc = bass.Bass("TRN2", target_bir_lowering=False)
    S = ntiles * T
    dr = lambda name, shape, kind="ExternalInput": nc.dram_tensor(name, shape, F32, kind=kind).ap()
    xT_d = dr("xT", [D, S])
    par_d = dr("par", [128, NPAR])
    bs_d = dr("bs", [1, 2 * 8 * 128])
    wsT_d = dr("wsT", [2, 8, 128, 128])
    mod_w = dr("mod_w", [DEPTH, D, 3 * D])
    a_w_in = dr("a_w_in", [2, D, 3 * E])
    a_w_out = dr("a_w_out", [2, E, D])
    b_w_in = dr("b_w_in", [2, D, 2 * E])
    b_ga_w = dr("b_ga_w", [2, 8, 256, 256])
    b_gx_w = dr("b_gx_w", [2, 8, 256, 256])
    b_w_out = dr("b_w_out", [2, E, D])
    oT_d = dr("oT", [D, S], kind="ExternalOutput")

    with ExitStack() as st:
        sb = lambda name, shape, dt: st.enter_context(nc.sbuf_tensor("s_" + name, shape, dt))
        P = Prog(nc)

        xT = [sb("xTs%d" % i, [128, KC, T], F32) for i in range(2)]
        hT = sb("hT", [128, KC, T], BF16)
        sq = sb("sq", [128, KC, T], BF16)
        oT = sb("oTs", [128, KC, T], F32)
        yT = sb("yT", [128, EC, T], BF16)
        ring = [sb("ring%d" % i, [128, 4096], BF16) for i in range(NSLOT)]
        NSC = 24
        scr = sb("scr", [128, NSC * 512], F32)
        ptab = sb("ptab", [128, NPAR], F32)
        bS = sb("bS", [128, 2048], F32)
        wsT = sb("wsT", [128, 16, 128], BF16)
        ones_bf = sb("ones_bf", [128, 128], BF16)
        one_f = sb("one_f", [1, 1], F32)
        mhalf1 = sb("mhalf", [128, 1], F32)
        phalf1 = sb("phalf", [128, 1], F32)

        def bc512(t1):
            a = t1[:]
            return bass.AP(a.tensor, a.offset, [list(a.ap[0]), [0, 512]])
        cond_bf = sb("cond_bf", [128, KC], BF16)
        ident_f = sb("ident_f", [128, 128], F32)
        ones_f = sb("ones_f", [128, 128], F32)
        diag = sb("diag", [128, 4, 128], F32)
        r4 = sb("r4", [128, 4], F32)
        modT = sb("modT", [128, 24], F32)
        Acoef = sb("Acoef", [128, DEPTH, KC], F32)
        Bcoef = sb("Bcoef", [128, DEPTH, KC], F32)
        Gcoef = sb("Gcoef", [128, DEPTH, KC], F32)
        Kc = sb("Kc", [128, 2, EC], F32)
        gabh = sb("gabh", [128, 2, EC], F32)
        gxbh = sb("gxbh", [128, 2, EC], F32)
        lrutmp = sb("lrutmp", [128, 2, EC], F32)
        ctail = sb("ctail", [128, 2, EC, 4], F32)
        hstate = sb("hstate", [128, 2, EC], F32)
        rbc = sb("rbc", [128, T], F32)
        stats = sb("stats", [128, 2, 4, 6], F32)
        mv = sb("mv", [128, 2, 2], F32)
        rs1 = sb("rs1", [128, 2, 2], F32)
        PS = [st.enter_context(nc.psum_tensor("ps%d" % i, [128, 512], F32)) for i in range(8)]

        modrow = scr[0:1, 16 * 512:16 * 512 + 3 * D]

        def sc(i, n=1):
            return scr[:, i * 512:(i + n) * 512]

        def sck(i, n=1):
            r = []
            for k in range(i, i + n):
                r += ["sc%da" % k, "sc%db" % k]
            return r

        bank_ctr = [0]

        def nb():
            b = bank_ctr[0] % 8
            bank_ctr[0] += 1
            return b

        A = P.add

        stream = []

        def full_chunk(src3):
            k, n = src3.shape[1], src3.shape[2]
            return [(lambda s, k=k, n=n: ring[s][:, 0:k * n].rearrange("p (k n) -> p k n", k=k), src3)]

        def win_chunk(w, col0, ncol):
            return full_chunk(w[:, col0:col0 + ncol].rearrange("(k p) n -> p k n", p=128))

        def wout_chunk(w, col0):
            return full_chunk(w[:, col0:col0 + 256].rearrange("(k p) n -> p k n", p=128))

        def gate_chunk(lb, q):
            specs = []
            for gi, gw in enumerate((b_ga_w, b_gx_w)):
                for h2 in range(2):
                    src = gw[lb, 2 * q + h2].rearrange("(i p) n -> p i n", p=128)
                    off = ((h2 * 2 + gi) * 2) * 256
                    specs.append((lambda s, off=off: ring[s][:, off:off + 512].rearrange("p (i n) -> p i n", i=2), src))
            return specs

        for ti in range(ntiles):
            for l in layers:
                j = l // 2
                if ti == 0:
                    for fc in range(6):
                        stream.append(win_chunk(mod_w[l], fc * 512, 512))
                if l % 2 == 0:
                    for fc in range(4):
                        stream.append(win_chunk(a_w_in[j], E + fc * 512, 512))
                    for q in range(4):
                        stream.append(win_chunk(a_w_in[j], q * 512, 512))
                        stream.append(win_chunk(a_w_in[j], 2 * E + q * 512, 512))
                    for q in range(4):
                        stream.append(wout_chunk(a_w_out[j], q * 256))
                else:
                    for q in range(4):
                        stream.append(win_chunk(b_w_in[j], q * 512, 512))
                        stream.append(win_chunk(b_w_in[j], E + q * 512, 512))
                        stream.append(gate_chunk(j, q))
                    for q in range(4):
                        stream.append(wout_chunk(b_w_out[j], q * 256))

        rstate = {"next_load": 0, "next_use": 0}

        def slotk(s):
            return ["slot%d_%d" % (s, k) for k in range(4)]

        def ring_load(ci):
            s = ci % NSLOT
            specs = stream[ci]
            for k, (dst_fn, src) in enumerate(specs):
                wk = slotk(s) if len(specs) == 1 else ["slot%d_%d" % (s, k)]
                A("pool", lambda h, dst_fn=dst_fn, src=src, s=s: h.dma_start(out=dst_fn(s), in_=src),
                  writes=wk, dma_key="slot%d" % s)

        def ring_use():
            ci = rstate["next_use"]
            rstate["next_use"] += 1
            assert ci < rstate["next_load"], "ring underflow"
            return ci, ci % NSLOT

        def ring_done(ci):
            nl = rstate["next_load"]
            assert nl == ci + NSLOT or nl >= len(stream), (nl, ci)
            if nl < len(stream):
                ring_load(nl)
                rstate["next_load"] += 1

        A("sp", lambda h: h.dma_start(out=ptab[:], in_=par_d[:]), writes=["ptab"], dma_key="ptab")
        A("sp", lambda h: h.dma_start(out=bS[:], in_=bs_d.partition_broadcast(128)), writes=["bS"], dma_key="bS")
        wsraw = scr[:, 0:2048].rearrange("p (a b) -> p a b", a=16)
        A("sp", lambda h: h.dma_start(out=wsraw, in_=wsT_d.rearrange("j g s t -> s (j g) t")), writes=sck(0, 4), dma_key="wsraw")
        for ci in range(min(NSLOT, len(stream))):
            ring_load(ci)
            rstate["next_load"] += 1
        A("dve", lambda h: h.memset(ones_bf[:], 1.0), writes=["ones_bf"])
        A("dve", lambda h: h.memset(one_f[:], 1.0), writes=["one_f"])
        A("dve", lambda h: h.memset(ones_f[:], 1.0), writes=["ones_f"])
        A("pool", lambda h: h.memset(ident_f[:], 1.0), writes=["ident_f"])
        A("pool", lambda h: h.affine_select(out=ident_f[:], in_=ident_f[:], pattern=[[1, 128]], compare_op=ALU.is_equal, fill=0.0, base=0, channel_multiplier=-1),
          reads=["ident_f"], writes=["ident_f"])
        A("dve", lambda h: h.memset(mhalf1[:], -0.5), writes=["mhalf"])
        A("dve", lambda h: h.memset(phalf1[:], 0.5), writes=["phalf"])
        A("dve", lambda h: h.memset(ctail[:], 0.0), writes=["ctail"])
        A("dve", lambda h: h.memset(hstate[:], 0.0), writes=["hstate"])
        A("pool", lambda h: h.affine_select(out=wsT[:], in_=wsraw, pattern=[[0, 16], [1, 128]],
                                            compare_op=ALU.is_ge, fill=0.0, base=0, channel_multiplier=-1),
          reads=sck(0, 4), writes=["wsT"])
        A("act", lambda h: h.activation(out=cond_bf[:], in_=ptab[:, P_C:P_C + 8], func=AF.Silu), reads=["ptab"], writes=["cond"])
        lam_v = ptab[:, P_LAM:P_LAM + 32].rearrange("p (a b) -> p a b", a=2)
        A("act", lambda h: h.activation(out=lrutmp[:], in_=lam_v, func=AF.Exp, scale=-1.0), reads=["ptab"], writes=["lrutmp"])
        A("act", lambda h: h.activation(out=Kc[:], in_=lrutmp[:], func=AF.Ln, bias=1.0, scale=1.0), reads=["lrutmp"], writes=["Kc"])
        A("dve", lambda h: h.tensor_scalar(out=Kc[:], in0=Kc[:], scalar1=-8.0, scalar2=None, op0=ALU.mult), reads=["Kc"], writes=["Kc"])
        A("dve", lambda h: h.tensor_scalar(out=gabh[:], in0=ptab[:, P_GAB:P_GAB + 32].rearrange("p (a b) -> p a b", a=2),
                                           scalar1=0.5, scalar2=None, op0=ALU.mult), reads=["ptab"], writes=["gabh"])
        A("dve", lambda h: h.tensor_scalar(out=gxbh[:], in0=ptab[:, P_GXB:P_GXB + 32].rearrange("p (a b) -> p a b", a=2),
                                           scalar1=0.5, scalar2=None, op0=ALU.mult), reads=["ptab"], writes=["gxbh"])

        def rms_bcast(src_sq_key, dst, dst_key):
            b = nb()
            for tc in range(TC):
                for kc in range(KC):
                    A("pe", lambda h, kc=kc, tc=tc, b=b: h.matmul(PS[b][:, tc:tc + 1], lhsT=sq[:, kc, tc * 128:(tc + 1) * 128], rhs=ones_bf[:, 0:1],
                                                                  start=(kc == 0), stop=(kc == KC - 1)),
                      reads=["ones_bf", src_sq_key + str(kc)], writes=["ps%d" % b])
            A("dve", lambda h, b=b: h.tensor_scalar(out=r4[:], in0=PS[b][:, 0:4], scalar1=1.0 / D, scalar2=EPS, op0=ALU.mult, op1=ALU.add),
              reads=["ps%d" % b], writes=["r4"])
            mh4 = bass.AP(mhalf1[:].tensor, mhalf1[:].offset, [list(mhalf1[:].ap[0]), [0, 4]])
            A("pool", lambda h: h.tensor_tensor(out=r4[:], in0=r4[:], in1=mh4, op=ALU.pow), reads=["r4", "mhalf"], writes=["r4"])
            b2 = nb()
            for tc in range(TC):
                A("dve", lambda h, tc=tc: h.tensor_scalar(out=diag[:, tc, :], in0=ident_f[:], scalar1=r4[:, tc:tc + 1], scalar2=None, op0=ALU.mult),
                  reads=["ident_f", "r4"], writes=["diag%d" % tc])
                A("pe", lambda h, tc=tc, b2=b2: h.matmul(PS[b2][:, tc * 128:(tc + 1) * 128], lhsT=ones_f[:], rhs=diag[:, tc, :], start=True, stop=True),
                  reads=["ones_f", "diag%d" % tc], writes=["ps%d" % b2])
            A("act", lambda h, b2=b2: h.activation(out=dst, in_=PS[b2][:], func=AF.Copy), reads=["ps%d" % b2], writes=[dst_key])

        def mod_compute(l):
            for fc in range(6):
                ci, s = ring_use()
                b = nb()
                for kc in range(KC):
                    A("pe", lambda h, kc=kc, b=b, s=s: h.matmul(PS[b][0:1, :], lhsT=cond_bf[:, kc:kc + 1], rhs=ring[s][:, kc * 512:(kc + 1) * 512],
                                                                start=(kc == 0), stop=(kc == KC - 1)),
                      reads=["cond", ] + slotk(s), writes=["ps%d" % b])
                A("act", lambda h, b=b, fc=fc: h.activation(out=modrow[0:1, fc * 512:(fc + 1) * 512], in_=PS[b][0:1, :], func=AF.Copy),
                  reads=["ps%d" % b], writes=sck(16, 6))
                ring_done(ci)
            b = nb()
            for jj in range(24):
                A("pe", lambda h, jj=jj, b=b: h.matmul(PS[b][:, jj:jj + 1], lhsT=modrow[0:1, jj * 128:(jj + 1) * 128], rhs=one_f[0:1, 0:1],
                                                       start=True, stop=True),
                  reads=sck(16, 6) + ["one_f"], writes=["ps%d" % b])
            A("dve", lambda h, b=b: h.tensor_tensor(out=modT[:], in0=PS[b][:, 0:24], in1=ptab[:, P_MODB + l * 24:P_MODB + (l + 1) * 24], op=ALU.add),
              reads=["ps%d" % b, "ptab"], writes=["modT"])
            A("dve", lambda h: h.scalar_tensor_tensor(out=Acoef[:, l, :], in0=modT[:, 8:16], scalar=1.0, in1=ptab[:, P_PRE + l * 8:P_PRE + (l + 1) * 8],
                                                      op0=ALU.add, op1=ALU.mult), reads=["modT", "ptab"], writes=["coef%d" % l])
            A("dve", lambda h: h.tensor_copy(out=Bcoef[:, l, :], in_=modT[:, 0:8]), reads=["modT"], writes=["coef%d" % l])
            A("dve", lambda h: h.tensor_tensor(out=Gcoef[:, l, :], in0=modT[:, 16:24], in1=ptab[:, P_POST + l * 8:P_POST + (l + 1) * 8], op=ALU.mult),
              reads=["modT", "ptab"], writes=["coef%d" % l])

        def pre_norm(l, X, xk):
            for kc in range(KC):
                A("act", lambda h, kc=kc: h.activation(out=sq[:, kc, :], in_=X[:, kc, :], func=AF.Square), reads=[xk + str(kc)], writes=["sq%d" % kc])
            rms_bcast("sq", rbc[:], "rbc")
            for kc in range(KC):
                t = sc(20 + kc % 2)
                tk = sck(20 + kc % 2)
                A("dve", lambda h, kc=kc, t=t: h.tensor_tensor(out=t, in0=X[:, kc, :], in1=rbc[:], op=ALU.mult), reads=[xk + str(kc), "rbc"], writes=tk)
                A("act", lambda h, kc=kc, t=t: h.activation(out=hT[:, kc, :], in_=t, func=AF.Identity, scale=Acoef[:, l, kc:kc + 1], bias=Bcoef[:, l, kc:kc + 1]),
                  reads=tk + ["coef%d" % l], writes=["hT%d" % kc])

        def out_proj_post(l, X, xk, slots_fn):
            ci = s = None
            for fo in range(KC):
                if fo % 2 == 0:
                    ci, s = ring_use()
                b = nb()
                for kc in range(EC):
                    A("pe", lambda h, kc=kc, b=b, s=s, fo=fo: h.matmul(PS[b][:], lhsT=ring[s][:, kc * 256 + (fo % 2) * 128:kc * 256 + (fo % 2) * 128 + 128],
                                                                       rhs=yT[:, kc, :], start=(kc == 0), stop=(kc == EC - 1)),
                      reads=slotk(s) + ["yT%d" % kc], writes=["ps%d" % b])
                A("act", lambda h, b=b, fo=fo: h.activation(out=oT[:, fo, :], in_=PS[b][:], func=AF.Copy), reads=["ps%d" % b], writes=["oT%d" % fo])
                A("act", lambda h, b=b, fo=fo: h.activation(out=sq[:, fo, :], in_=PS[b][:], func=AF.Square), reads=["ps%d" % b], writes=["sq%d" % fo])
                if fo % 2 == 1:
                    ring_done(ci)
            rms_bcast("sq", rbc[:], "rbc")
            for fo in range(KC):
                t = sc(20 + fo % 2)
                tk = sck(20 + fo % 2)
                A("dve", lambda h, fo=fo, t=t: h.tensor_tensor(out=t, in0=oT[:, fo, :], in1=rbc[:], op=ALU.mult), reads=["oT%d" % fo, "rbc"], writes=tk)
                A("dve", lambda h, fo=fo, t=t: h.scalar_tensor_tensor(out=X[:, fo, :], in0=t, scalar=Gcoef[:, l, fo:fo + 1], in1=X[:, fo, :],
                                                                      op0=ALU.mult, op1=ALU.add),
                  reads=tk + ["coef%d" % l, xk + str(fo)], writes=[xk + str(fo)])

        def layer_a(l, X, xk):
            j = l // 2
            pre_norm(l, X, xk)
            vslots = [ring_use() for _ in range(4)]
            vn = scr[:, 8 * 512:16 * 512].bitcast(BF16)
            for tc in range(TC):
                gi = tc % 2
                gv = sc(gi * 4, 4)
                for fc in range(4):
                    ci, s = vslots[fc]
                    b = nb()
                    for kc in range(KC):
                        A("pe", lambda h, kc=kc, b=b, s=s, tc=tc: h.matmul(PS[b][:], lhsT=hT[:, kc, tc * 128:(tc + 1) * 128], rhs=ring[s][:, kc * 512:(kc + 1) * 512],
                                                                           start=(kc == 0), stop=(kc == KC - 1)),
                          reads=["hT%d" % kc, ] + slotk(s), writes=["ps%d" % b])
                    A("act", lambda h, b=b, fc=fc, gv=gv: h.activation(out=gv[:, fc * 512:(fc + 1) * 512], in_=PS[b][:], func=AF.Gelu_apprx_tanh),
                      reads=["ps%d" % b], writes=sck(gi * 4 + fc))
                    A("dve", lambda h, fc=fc, gv=gv, gi=gi: h.bn_stats(out=stats[:, gi, fc, :], in_=gv[:, fc * 512:(fc + 1) * 512]),
                      reads=sck(gi * 4 + fc), writes=["stats%d" % gi])
                A("dve", lambda h, gi=gi: h.bn_aggr(out=mv[:, gi, :], in_=stats[:, gi, :, :].rearrange("p a b -> p (a b)")), reads=["stats%d" % gi], writes=["mv%d" % gi])
                A("dve", lambda h, gi=gi: h.tensor_scalar(out=rs1[:, gi, 0:1], in0=mv[:, gi, 1:2], scalar1=EPS, scalar2=None, op0=ALU.add),
                  reads=["mv%d" % gi], writes=["rs%d" % gi])
                A("pool", lambda h, gi=gi: h.tensor_tensor(out=rs1[:, gi, 1:2], in0=rs1[:, gi, 0:1], in1=mhalf1[:], op=ALU.pow),
                  reads=["rs%d" % gi, "mhalf"], writes=["rs%d" % gi])
                A("dve", lambda h, gi=gi, gv=gv, tc=tc: h.tensor_scalar(out=vn[:, tc * 2048:(tc + 1) * 2048], in0=gv, scalar1=mv[:, gi, 0:1], scalar2=rs1[:, gi, 1:2],
                                                                        op0=ALU.subtract, op1=ALU.mult),
                  reads=sck(gi * 4, 4) + ["mv%d" % gi, "rs%d" % gi], writes=sck(8 + tc * 2, 2))
            for (ci, s) in vslots:
                ring_done(ci)

            def mix(cc):
                g = cc // 2
                b = nb()
                for tc in range(TC):
                    A("pe", lambda h, tc=tc, b=b: h.matmul(PS[b][:, tc * 128:(tc + 1) * 128], lhsT=vn[:, tc * 2048 + cc * 128:tc * 2048 + (cc + 1) * 128],
                                                           rhs=wsT[:, j * 8 + g, :], start=True, stop=True),
                      reads=sck(8 + tc * 2, 2) + ["wsT"], writes=["ps%d" % b])
                return b

            pend = None
            uci = gci = us = gs = None
            for cc in range(EC + 1):
                if cc < EC:
                    if cc % 4 == 0:
                        uci, us = ring_use()
                        gci, gs = ring_use()
                    off = (cc % 4) * 128
                    bu = nb()
                    for kc in range(KC):
                        A("pe", lambda h, kc=kc, b=bu, s=us, off=off: h.matmul(PS[b][:], lhsT=ring[s][:, kc * 512 + off:kc * 512 + off + 128], rhs=hT[:, kc, :],
                                                                               start=(kc == 0), stop=(kc == KC - 1)),
                          reads=[] + slotk(us) + ["hT%d" % kc], writes=["ps%d" % bu])
                    bg = nb()
                    for kc in range(KC):
                        A("pe", lambda h, kc=kc, b=bg, s=gs, off=off: h.matmul(PS[b][:], lhsT=ring[s][:, kc * 512 + off:kc * 512 + off + 128], rhs=hT[:, kc, :],
                                                                               start=(kc == 0), stop=(kc == KC - 1)),
                          reads=[] + slotk(gs) + ["hT%d" % kc], writes=["ps%d" % bg])
                    if cc % 4 == 3:
                        ring_done(uci)
                        ring_done(gci)
                    par = cc % 2
                    t_gu, t_th, t_sg, t_m = 16 + par * 2, 17 + par * 2, 0 + par * 2, 1 + par * 2
                    A("act", lambda h, b=bu, t=t_gu: h.activation(out=sc(t), in_=PS[b][:], func=AF.Gelu_apprx_tanh), reads=["ps%d" % bu], writes=sck(t_gu))
                    A("act", lambda h, b=bg, t=t_th: h.activation(out=sc(t), in_=PS[b][:], func=AF.Tanh, scale=0.5), reads=["ps%d" % bg], writes=sck(t_th))
                    A("dve", lambda h, b=bg, t=t_th, o=t_sg: h.scalar_tensor_tensor(out=sc(o), in0=sc(t), scalar=1.0, in1=PS[b][:], op0=ALU.add, op1=ALU.mult),
                      reads=sck(t_th) + ["ps%d" % bg], writes=sck(t_sg))
                    A("dve", lambda h, a=t_gu, o=t_sg: h.tensor_tensor(out=sc(o), in0=sc(a), in1=sc(o), op=ALU.mult), reads=sck(t_gu) + sck(t_sg), writes=sck(t_sg))
                    cur = (cc, t_sg, t_m)
                else:
                    cur = None
                if pend is not None:
                    pcc, p_sg, p_m = pend
                    bm = mix(pcc)
                    g = pcc // 2
                    bs_ap = bass.AP(bS[:].tensor, bS[:, (j * 8 + g) * 128:(j * 8 + g) * 128 + 128].offset, [list(bS[:].ap[0]), [0, 4], [1, 128]])
                    A("dve", lambda h, b=bm, o=p_m, pcc=pcc, bs_ap=bs_ap: h.scalar_tensor_tensor(
                        out=sc(o).rearrange("p (a b) -> p a b", a=4), in0=PS[b][:].rearrange("p (a b) -> p a b", a=4),
                        scalar=ptab[:, P_VN + j * 16 + pcc:P_VN + j * 16 + pcc + 1], in1=bs_ap, op0=ALU.mult, op1=ALU.add),
                      reads=["ps%d" % bm, "ptab", "bS"], writes=sck(p_m))
                    A("dve", lambda h, o=p_m, s_=p_sg, pcc=pcc: h.scalar_tensor_tensor(out=yT[:, pcc, :], in0=sc(o), scalar=0.5, in1=sc(s_), op0=ALU.mult, op1=ALU.mult),
                      reads=sck(p_m) + sck(p_sg), writes=["yT%d" % pcc])
                pend = cur
            out_proj_post(l, X, xk, None)

        def layer_b(l, X, xk):
            lb = l // 2
            pre_norm(l, X, xk)
            def xbuf(hp, ci, a=0, n=515):
                o = (hp * 2 + ci) * 1024
                return scr[:, o + a:o + a + n]
            xbk = lambda hp, ci: sck((hp * 2 + ci) * 2, 2)
            xb_all = xbk
            XC = 8
            XCB = 12
            xcb_all = scr[:, XCB * 512:(XCB + 2) * 512].bitcast(BF16)
            xcbk = lambda hp, ci: ["sc%d%s" % (XCB + (hp * 2 + ci) // 2, "ab"[(hp * 2 + ci) % 2])]
            TH = 14
            state = {"bank_x": {}, "bank_g": {}}
            xq = {}

            def pe_xb(hd):
                hp = hd % 2
                if hd % 2 == 0:
                    xq["x"] = ring_use()
                ci_, s = xq["x"]
                for ci in range(2):
                    b = nb()
                    off = ((hd % 2) * 2 + ci) * 128
                    for kc in range(KC):
                        A("pe", lambda h, kc=kc, b=b, s=s, off=off: h.matmul(PS[b][:], lhsT=ring[s][:, kc * 512 + off:kc * 512 + off + 128], rhs=hT[:, kc, :],
                                                                             start=(kc == 0), stop=(kc == KC - 1)),
                          reads=slotk(s) + ["hT%d" % kc], writes=["ps%d" % b])
                    cc = hd * 2 + ci
                    A("dve", lambda h, hp=hp, ci=ci, cc=cc: h.tensor_copy(out=xbuf(hp, ci, 0, 3), in_=ctail[:, lb, cc, 0:3]),
                      reads=["ctail%d_%d" % (lb, cc)], writes=xb_all(hp, ci))
                    A("act", lambda h, b=b, hp=hp, ci=ci: h.activation(out=xbuf(hp, ci, 3, 512), in_=PS[b][:], func=AF.Copy),
                      reads=["ps%d" % b], writes=xb_all(hp, ci))
                    A("dve", lambda h, hp=hp, ci=ci, cc=cc: h.tensor_copy(out=ctail[:, lb, cc, 0:3], in_=xbuf(hp, ci, 512, 3)),
                      reads=xbk(hp, ci), writes=["ctail%d_%d" % (lb, cc)])
                    xc = sc(XC + hp * 2 + ci)
                    xck = sck(XC + hp * 2 + ci)
                    cw = lambda k, cc=cc: ptab[:, P_CW + (lb * 4 + k) * 16 + cc:P_CW + (lb * 4 + k) * 16 + cc + 1]
                    A("act", lambda h, b=b, xc=xc, cc=cc, cw=cw: h.activation(out=xc, in_=PS[b][:], func=AF.Identity, scale=cw(3),
                                                                             bias=ptab[:, P_CB + lb * 16 + cc:P_CB + lb * 16 + cc + 1]),
                      reads=["ps%d" % b, "ptab"], writes=xck)
                    for k in (2, 1, 0):
                        A("dve", lambda h, k=k, xc=xc, hp=hp, ci=ci, cw=cw: h.scalar_tensor_tensor(out=xc, in0=xbuf(hp, ci, k, 512), scalar=cw(k), in1=xc,
                                                                                                   op0=ALU.mult, op1=ALU.add),
                          reads=xbk(hp, ci) + xck + ["ptab"], writes=xck)
                    A("act", lambda h, xc=xc, hp=hp, ci=ci: h.activation(out=xcb_all[:, (hp * 2 + ci) * 512:(hp * 2 + ci + 1) * 512], in_=xc, func=AF.Copy),
                      reads=xck, writes=xcbk(hp, ci))
                if hd % 2 == 1:
                    ring_done(ci_)

            def pe_g(hd):
                if hd % 2 == 0:
                    xq["g"] = ring_use()
                ci_, s = xq["g"]
                banks = []
                for ci in range(2):
                    b = nb()
                    off = ((hd % 2) * 2 + ci) * 128
                    for kc in range(KC):
                        A("pe", lambda h, kc=kc, b=b, s=s, off=off: h.matmul(PS[b][:], lhsT=ring[s][:, kc * 512 + off:kc * 512 + off + 128], rhs=hT[:, kc, :],
                                                                             start=(kc == 0), stop=(kc == KC - 1)),
                          reads=slotk(s) + ["hT%d" % kc], writes=["ps%d" % b])
                    t = 20 + (hd % 2) * 2 + ci
                    A("act", lambda h, b=b, t=t: h.activation(out=sc(t), in_=PS[b][:], func=AF.Tanh, scale=0.5), reads=["ps%d" % b], writes=sck(t))
                    A("dve", lambda h, b=b, t=t: h.scalar_tensor_tensor(out=sc(t), in0=sc(t), scalar=1.0, in1=PS[b][:], op0=ALU.add, op1=ALU.mult),
                      reads=sck(t) + ["ps%d" % b], writes=sck(t))
                if hd % 2 == 1:
                    ring_done(ci_)

            def pe_gates(hd):
                hp = hd % 2
                if hd % 2 == 0:
                    xq["w"] = ring_use()
                ci_, s = xq["w"]
                h2 = hd % 2
                for jc in range(2):
                    cc = hd * 2 + jc
                    br = nb()
                    bi = nb()
                    for gi, b in ((0, br), (1, bi)):
                        for ic in range(2):
                            woff = ((h2 * 2 + gi) * 2 + ic) * 256 + jc * 128
                            A("pe", lambda h, b=b, s=s, woff=woff, ic=ic, hp=hp: h.matmul(PS[b][:], lhsT=ring[s][:, woff:woff + 128],
                                                                                         rhs=xcb_all[:, (hp * 2 + ic) * 512:(hp * 2 + ic + 1) * 512],
                                                                                         start=(ic == 0), stop=(ic == 1)),
                              reads=slotk(s) + xcbk(hp, ic), writes=["ps%d" % b])
                    t_r, t_i, t_a2 = TH + jc * 3, TH + jc * 3 + 1, TH + jc * 3 + 2
                    kcol = Kc[:, lb, cc:cc + 1]
                    A("act", lambda h, b=br, t=t_r, cc=cc: h.activation(out=sc(t), in_=PS[b][:], func=AF.Tanh, scale=0.5, bias=gabh[:, lb, cc:cc + 1]),
                      reads=["ps%d" % br, "gabh"], writes=sck(t_r))
                    A("act", lambda h, b=bi, t=t_i, cc=cc: h.activation(out=sc(t), in_=PS[b][:], func=AF.Tanh, scale=0.5, bias=gxbh[:, lb, cc:cc + 1]),
                      reads=["ps%d" % bi, "gxbh"], writes=sck(t_i))
                    A("act", lambda h, t=t_r, o=t_a2, kcol=kcol: h.activation(out=sc(o), in_=sc(t), func=AF.Exp, scale=kcol, bias=kcol),
                      reads=sck(t_r) + ["Kc"], writes=sck(t_a2))
                    xc = sc(XC + hp * 2 + jc)
                    xck = sck(XC + hp * 2 + jc)
                    A("dve", lambda h, t=t_i, xc=xc: h.scalar_tensor_tensor(out=sc(t), in0=sc(t), scalar=1.0, in1=xc, op0=ALU.add, op1=ALU.mult),
                      reads=sck(t_i) + xck, writes=sck(t_i))
                for jc in range(2):
                    cc = hd * 2 + jc
                    t_r, t_i, t_a2 = TH + jc * 3, TH + jc * 3 + 1, TH + jc * 3 + 2
                    A("act", lambda h, o=t_a2, r=t_r: h.activation(out=sc(r), in_=sc(o), func=AF.Sqrt, scale=-1.0, bias=1.0), reads=sck(t_a2), writes=sck(t_r))
                    A("act", lambda h, o=t_a2: h.activation(out=sc(o), in_=sc(o), func=AF.Sqrt), reads=sck(t_a2), writes=sck(t_a2))
                    A("dve", lambda h, t=t_i, r=t_r: h.scalar_tensor_tensor(out=sc(t), in0=sc(t), scalar=0.5, in1=sc(r), op0=ALU.mult, op1=ALU.mult),
                      reads=sck(t_i) + sck(t_r), writes=sck(t_i))
                    A("dve", lambda h, o=t_a2, t=t_i, r=t_r, cc=cc: h.tensor_tensor_scan(out=sc(r), data0=sc(o), data1=sc(t), initial=hstate[:, lb, cc:cc + 1],
                                                                                        op0=ALU.mult, op1=ALU.add),
                      reads=sck(t_a2) + sck(t_i) + ["hst%d_%d" % (lb, cc)], writes=sck(t_r))
                    A("dve", lambda h, r=t_r, cc=cc: h.tensor_copy(out=hstate[:, lb, cc:cc + 1], in_=sc(r)[:, 511:512]), reads=sck(t_r), writes=["hst%d_%d" % (lb, cc)])
                    tg = 20 + (hd % 2) * 2 + jc
                    A("dve", lambda h, r=t_r, tg=tg, cc=cc: h.scalar_tensor_tensor(out=yT[:, cc, :], in0=sc(r), scalar=0.5, in1=sc(tg), op0=ALU.mult, op1=ALU.mult),
                      reads=sck(t_r) + sck(tg), writes=["yT%d" % cc])
                if hd % 2 == 1:
                    ring_done(ci_)

            pe_xb(0)
            pe_xb(1)
            for hd in range(8):
                pe_g(hd)
                pe_gates(hd)
                if hd + 2 < 8:
                    pe_xb(hd + 2)
            out_proj_post(l, X, xk, None)

        x_loads = {}

        def load_x(ti):
            X = xT[ti % 2]
            xk = "x%d_" % (ti % 2)
            x_loads[ti] = A("sp", lambda h, ti=ti, X=X: h.dma_start(out=X[:], in_=xT_d[:, ti * T:(ti + 1) * T].rearrange("(k p) t -> p k t", p=128)),
                            writes=[xk + str(kc) for kc in range(KC)], dma_key="xin%d" % (ti % 2))

        out_ops = []
        load_x(0)
        for ti in range(ntiles):
            X = xT[ti % 2]
            xk = "x%d_" % (ti % 2)
            if ti + 1 < ntiles:
                load_x(ti + 1)
            for l in layers:
                if ti == 0:
                    mod_compute(l)
                if l % 2 == 0:
                    layer_a(l, X, xk)
                else:
                    layer_b(l, X, xk)
            out_ops.append(A("sp", lambda h, ti=ti, X=X: h.dma_start(out=oT_d[:, ti * T:(ti + 1) * T].rearrange("(k p) t -> p k t", p=128), in_=X[:]),
                             reads=[xk + str(kc) for kc in range(KC)], dma_key="xout%d" % (ti % 2)))
        assert rstate["next_use"] == len(stream), (rstate, len(stream))
        P.emit(final_waits=out_ops)
    return nc


_CACHE = {}


def _get_nc(layers, ntiles):
    key = (tuple(layers), ntiles)
    if key not in _CACHE:
        _CACHE[key] = build(layers, ntiles)
    return _CACHE[key]


def make_in_maps(inp, batches, ntiles, x_override=None):
    f = lambda a: np.ascontiguousarray(np.asarray(a, dtype=np.float32))
    rows = [f(inp["pre_norm"]).reshape(32, 128), f(inp["post_norm"]).reshape(32, 128), f(inp["mod_b"]).reshape(96, 128),
            f(inp["a_v_norm"]).reshape(32, 128), f(inp["b_conv_w"]).reshape(128, 128), f(inp["b_conv_b"]).reshape(32, 128),
            f(inp["b_ga_b"]).reshape(32, 128), f(inp["b_gx_b"]).reshape(32, 128), f(inp["b_lambda"]).reshape(32, 128)]
    shared = {
        "bs": f(inp["a_b_s"]).reshape(1, 2048),
        "wsT": np.ascontiguousarray(np.transpose(f(inp["a_w_s"]), (0, 1, 3, 2))),
        "mod_w": f(inp["mod_w"]), "a_w_in": f(inp["a_w_in"]), "a_w_out": f(inp["a_w_out"]),
        "b_w_in": f(inp["b_w_in"]), "b_ga_w": f(inp["b_ga_w"]), "b_gx_w": f(inp["b_gx_w"]), "b_w_out": f(inp["b_w_out"]),
    }
    S = ntiles * T
    maps = []
    for bi, b in enumerate(batches):
        par = np.concatenate(rows + [f(inp["c"])[b].reshape(8, 128)], axis=0)
        xb = f(inp["x"])[b] if x_override is None else x_override[bi]
        m = dict(shared)
        m["par"] = np.ascontiguousarray(par.T)
        m["xT"] = np.ascontiguousarray(xb[:S].T)
        maps.append(m)
    return maps


def kernel(**inputs):
    nb_ = 8
    nc = _get_nc((0, 1, 2, 3), SEQ // T)
    maps = make_in_maps(inputs, list(range(nb_)), SEQ // T)
    res = run_bass_kernel_spmd(nc, maps, core_ids=list(range(nb_)))
    out = np.stack([np.ascontiguousarray(r["oT"].T) for r in res.results], axis=0)
    return out.astype(np.float32)
```

```python
import numpy as np
from contextlib import ExitStack
import concourse.bass as bass
import concourse.mybir as mybir
from concourse.bass_utils import run_bass_kernel_spmd

F32 = mybir.dt.float32
BF16 = mybir.dt.bfloat16
AF = mybir.ActivationFunctionType
ALU = mybir.AluOpType
ENGS = ["pe", "act", "dve", "pool", "sp"]

D = 1024
KC = 8
E = 2048
EC = 16
T = 512
TC = 4
SEQ = 4096
DEPTH = 4
EPS = 1e-6
NSLOT = 6
NPAR = 456

P_PRE, P_POST, P_MODB, P_VN, P_CW, P_CB, P_GAB, P_GXB, P_LAM, P_C = 0, 32, 64, 160, 192, 320, 352, 384, 416, 448


class Op:
    __slots__ = ("eng", "fn", "deps", "pos", "sig", "dma_key", "dma_val", "needed")


class Prog:
    def __init__(self, nc):
        self.nc = nc
        self.ops = {e: [] for e in ENGS}
        self.last_w = {}
        self.readers = {}
        self.dma_cnt = {}
        self.nops = 0

    def add(self, eng, fn, reads=(), writes=(), deps=(), dma_key=None):
        op = Op()
        op.eng = eng
        op.fn = fn
        op.needed = False
        op.sig = None
        op.dma_key = dma_key
        op.dma_val = None
        d = set(x for x in deps if x is not None)
        for r in reads:
            w = self.last_w.get(r)
            if w is not None:
                d.add(w)
        for r in writes:
            w = self.last_w.get(r)
            if w is not None:
                d.add(w)
            for rd in self.readers.get(r, ()):
                d.add(rd)
        if dma_key is not None:
            self.dma_cnt[dma_key] = self.dma_cnt.get(dma_key, 0) + 16
            op.dma_val = self.dma_cnt[dma_key]
        for r in reads:
            self.readers.setdefault(r, []).append(op)
        for r in writes:
            self.last_w[r] = op
            self.readers[r] = []
        d.discard(op)
        if eng == "pe":
            d = set(x for x in d if not (x.eng == "pe" and x.dma_key is None))
        latest = {}
        keep = set()
        for x in d:
            if x.dma_key is not None:
                keep.add(x)
            elif x.eng not in latest or latest[x.eng].pos < x.pos:
                latest[x.eng] = x
        d = keep | set(latest.values())
        op.deps = d
        for x in d:
            x.needed = True
        self.ops[eng].append(op)
        op.pos = self.nops
        self.nops += 1
        return op

    def emit(self, final_waits=()):
        nc = self.nc
        for x in final_waits:
            x.needed = True
        for e in ENGS:
            c = 0
            for op in self.ops[e]:
                if op.dma_key is None and op.needed:
                    c += 1
                    op.sig = c
        with ExitStack() as st:
            esem = {e: st.enter_context(nc.semaphore("s_" + e)) for e in ENGS}
            dsem = {k: st.enter_context(nc.semaphore("d_%d" % i))
                    for i, k in enumerate(self.dma_cnt.keys())}
            block = st.enter_context(nc.Block())

            def run(ename, h):
                waited = {}

                def do_wait(x):
                    if x.dma_key is not None:
                        s, v, key = dsem[x.dma_key], x.dma_val, ("d", x.dma_key)
                    else:
                        s, v, key = esem[x.eng], x.sig, ("e", x.eng)
                    if waited.get(key, 0) >= v:
                        return
                    waited[key] = v
                    h.wait_ge(s, v)

                for op in self.ops[ename]:
                    for x in sorted(op.deps, key=lambda o: o.pos):
                        do_wait(x)
                    ins = op.fn(h)
                    if op.dma_key is not None:
                        ins.then_inc(dsem[op.dma_key], 16)
                    elif op.needed:
                        ins.then_inc(esem[ename], 1)
                if ename == "sp":
                    for x in final_waits:
                        do_wait(x)

            @block.tensor
            def _(h):
                run("pe", h)

            @block.scalar
            def _(h):
                run("act", h)

            @block.vector
            def _(h):
                run("dve", h)

            @block.gpsimd
            def _(h):
                run("pool", h)

            @block.sync
            def _(h):
                run("sp", h)


def build(layers=(0, 1, 2, 3), ntiles=8):
    nc = bass.Bass("TRN2", target_bir_lowering=False)
    S = ntiles * T
    dr = lambda name, shape, kind="ExternalInput": nc.dram_tensor(name, shape, F32, kind=kind).ap()
    xT_d = dr("xT", [D, S])
    par_d = dr("par", [128, NPAR])
    bs_d = dr("bs", [1, 2 * 8 * 128])
    wsT_d = dr("wsT", [2, 8, 128, 128])
    mod_w = dr("mod_w", [DEPTH, D, 3 * D])
    a_w_in = dr("a_w_in", [2, D, 3 * E])
    a_w_out = dr("a_w_out", [2, E, D])
    b_w_in = dr("b_w_in", [2, D, 2 * E])
    b_ga_w = dr("b_ga_w", [2, 8, 256, 256])
    b_gx_w = dr("b_gx_w", [2, 8, 256, 256])
    b_w_out = dr("b_w_out", [2, E, D])
    oT_d = dr("oT", [D, S], kind="ExternalOutput")

    with ExitStack() as st:
        sb = lambda name, shape, dt: st.enter_context(nc.sbuf_tensor("s_" + name, shape, dt))
        P = Prog(nc)

        xT = [sb("xTs%d" % i, [128, KC, T], F32) for i in range(2)]
        hT = sb("hT", [128, KC, T], BF16)
        xbufs = sb("xbufs", [128, 4, 516], F32)
        oT = sb("oTs", [128, KC, T], F32)
        yT = sb("yT", [128, EC, T], BF16)
        ring = [sb("ring%d" % i, [128, 4096], BF16) for i in range(NSLOT)]
        NSC = 24
        scr = sb("scr", [128, NSC * 512], F32)
        ptab = sb("ptab", [128, NPAR], F32)
        bS = sb("bS", [128, 2048], F32)
        wsT = sb("wsT", [128, 16, 128], BF16)
        ones_bf = sb("ones_bf", [128, 128], BF16)
        one_f = sb("one_f", [1, 1], F32)
        mhalf1 = sb("mhalf", [128, 1], F32)
        phalf1 = sb("phalf", [128, 1], F32)

        def bc512(t1):
            a = t1[:]
            return bass.AP(a.tensor, a.offset, [list(a.ap[0]), [0, 512]])
        cond_bf = sb("cond_bf", [128, KC], BF16)
        ident_f = sb("ident_f", [128, 128], F32)
        ones_f = sb("ones_f", [128, 128], F32)
        diag = sb("diag", [128, 4, 128], F32)
        r4 = sb("r4", [128, 4], F32)
        modT = sb("modT", [128, 24], F32)
        Acoef = sb("Acoef", [128, DEPTH, KC], F32)
        Bcoef = sb("Bcoef", [128, DEPTH, KC], F32)
        Gcoef = sb("Gcoef", [128, DEPTH, KC], F32)
        Kc = sb("Kc", [128, 2, EC], F32)
        gabh = sb("gabh", [128, 2, EC], F32)
        gxbh = sb("gxbh", [128, 2, EC], F32)
        lrutmp = sb("lrutmp", [128, 2, EC], F32)
        ctail = sb("ctail", [128, 2, EC, 4], F32)
        hstate = sb("hstate", [128, 2, EC], F32)
        rbc = sb("rbc", [128, T], F32)
        stats = sb("stats", [128, 2, 4, 6], F32)
        mv = sb("mv", [128, 2, 2], F32)
        rs1 = sb("rs1", [128, 2, 2], F32)
        PS = [st.enter_context(nc.psum_tensor("ps%d" % i, [128, 512], F32)) for i in range(8)]

        modrow = scr[0:1, 16 * 512:16 * 512 + 3 * D]

        def sc(i, n=1):
            return scr[:, i * 512:(i + n) * 512]

        def sck(i, n=1):
            r = []
            for k in range(i, i + n):
                r += ["sc%da" % k, "sc%db" % k]
            return r

        bank_ctr = [0]

        def nb():
            b = bank_ctr[0] % 8
            bank_ctr[0] += 1
            return b

        A = P.add

        stream = []

        def full_chunk(src3):
            k, n = src3.shape[1], src3.shape[2]
            return [(lambda s, k=k, n=n: ring[s][:, 0:k * n].rearrange("p (k n) -> p k n", k=k), src3)]

        def win_chunk(w, col0, ncol):
            return full_chunk(w[:, col0:col0 + ncol].rearrange("(k p) n -> p k n", p=128))

        def wout_chunk(w, col0):
            return full_chunk(w[:, col0:col0 + 256].rearrange("(k p) n -> p k n", p=128))

        def gate_chunk(lb, q):
            specs = []
            for gi, gw in enumerate((b_ga_w, b_gx_w)):
                for h2 in range(2):
                    src = gw[lb, 2 * q + h2].rearrange("(i p) n -> p i n", p=128)
                    off = ((h2 * 2 + gi) * 2) * 256
                    specs.append((lambda s, off=off: ring[s][:, off:off + 512].rearrange("p (i n) -> p i n", i=2), src))
            return specs

        for ti in range(ntiles):
            for l in layers:
                j = l // 2
                if ti == 0:
                    for fc in range(6):
                        stream.append(win_chunk(mod_w[l], fc * 512, 512))
                if l % 2 == 0:
                    for fc in range(4):
                        stream.append(win_chunk(a_w_in[j], E + fc * 512, 512))
                    for q in range(4):
                        stream.append(win_chunk(a_w_in[j], q * 512, 512))
                        stream.append(win_chunk(a_w_in[j], 2 * E + q * 512, 512))
                    for q in range(4):
                        stream.append(wout_chunk(a_w_out[j], q * 256))
                else:
                    for q in range(4):
                        stream.append(win_chunk(b_w_in[j], q * 512, 512))
                        stream.append(win_chunk(b_w_in[j], E + q * 512, 512))
                        stream.append(gate_chunk(j, q))
                    for q in range(4):
                        stream.append(wout_chunk(b_w_out[j], q * 256))

        rstate = {"next_load": 0, "next_use": 0}

        def slotk(s):
            return ["slot%d_%d" % (s, k) for k in range(4)]

        def ring_load(ci):
            s = ci % NSLOT
            specs = stream[ci]
            for k, (dst_fn, src) in enumerate(specs):
                wk = slotk(s) if len(specs) == 1 else ["slot%d_%d" % (s, k)]
                A("pool", lambda h, dst_fn=dst_fn, src=src, s=s: h.dma_start(out=dst_fn(s), in_=src),
                  writes=wk, dma_key="slot%d" % s)

        def ring_use():
            ci = rstate["next_use"]
            rstate["next_use"] += 1
            assert ci < rstate["next_load"], "ring underflow"
            return ci, ci % NSLOT

        def ring_done(ci):
            nl = rstate["next_load"]
            assert nl == ci + NSLOT or nl >= len(stream), (nl, ci)
            if nl < len(stream):
                ring_load(nl)
                rstate["next_load"] += 1

        A("sp", lambda h: h.dma_start(out=ptab[:], in_=par_d[:]), writes=["ptab"], dma_key="ptab")
        A("sp", lambda h: h.dma_start(out=bS[:], in_=bs_d.partition_broadcast(128)), writes=["bS"], dma_key="bS")
        wsraw = scr[:, 0:2048].rearrange("p (a b) -> p a b", a=16)
        A("sp", lambda h: h.dma_start(out=wsraw, in_=wsT_d.rearrange("j g s t -> s (j g) t")), writes=sck(0, 4), dma_key="wsraw")
        for ci in range(min(NSLOT, len(stream))):
            ring_load(ci)
            rstate["next_load"] += 1
        A("dve", lambda h: h.memset(ones_bf[:], 1.0), writes=["ones_bf"])
        A("dve", lambda h: h.memset(one_f[:], 1.0), writes=["one_f"])
        A("dve", lambda h: h.memset(ones_f[:], 1.0), writes=["ones_f"])
        A("pool", lambda h: h.memset(ident_f[:], 1.0), writes=["ident_f"])
        A("pool", lambda h: h.affine_select(out=ident_f[:], in_=ident_f[:], pattern=[[1, 128]], compare_op=ALU.is_equal, fill=0.0, base=0, channel_multiplier=-1),
          reads=["ident_f"], writes=["ident_f"])
        A("dve", lambda h: h.memset(mhalf1[:], -0.5), writes=["mhalf"])
        A("dve", lambda h: h.memset(phalf1[:], 0.5), writes=["phalf"])
        A("dve", lambda h: h.memset(ctail[:], 0.0), writes=["ctail"])
        A("dve", lambda h: h.memset(hstate[:], 0.0), writes=["hstate"])
        A("pool", lambda h: h.affine_select(out=wsT[:], in_=wsraw, pattern=[[0, 16], [1, 128]],
                                            compare_op=ALU.is_ge, fill=0.0, base=0, channel_multiplier=-1),
          reads=sck(0, 4), writes=["wsT"])
        A("act", lambda h: h.activation(out=cond_bf[:], in_=ptab[:, P_C:P_C + 8], func=AF.Silu), reads=["ptab"], writes=["cond"])
        lam_v = ptab[:, P_LAM:P_LAM + 32].rearrange("p (a b) -> p a b", a=2)
        A("act", lambda h: h.activation(out=lrutmp[:], in_=lam_v, func=AF.Exp, scale=-1.0), reads=["ptab"], writes=["lrutmp"])
        A("act", lambda h: h.activation(out=Kc[:], in_=lrutmp[:], func=AF.Ln, bias=1.0, scale=1.0), reads=["lrutmp"], writes=["Kc"])
        A("dve", lambda h: h.tensor_scalar(out=Kc[:], in0=Kc[:], scalar1=-8.0, scalar2=None, op0=ALU.mult), reads=["Kc"], writes=["Kc"])
        A("dve", lambda h: h.tensor_scalar(out=gabh[:], in0=ptab[:, P_GAB:P_GAB + 32].rearrange("p (a b) -> p a b", a=2),
                                           scalar1=0.5, scalar2=None, op0=ALU.mult), reads=["ptab"], writes=["gabh"])
        A("dve", lambda h: h.tensor_scalar(out=gxbh[:], in0=ptab[:, P_GXB:P_GXB + 32].rearrange("p (a b) -> p a b", a=2),
                                           scalar1=0.5, scalar2=None, op0=ALU.mult), reads=["ptab"], writes=["gxbh"])

        def rms_bcast(sq, src_sq_key, dst, dst_key):
            b = nb()
            for tc in range(TC):
                for kc in range(KC):
                    A("pe", lambda h, kc=kc, tc=tc, b=b: h.matmul(PS[b][:, tc:tc + 1], lhsT=sq[:, kc, tc * 128:(tc + 1) * 128], rhs=ones_bf[:, 0:1],
                                                                  start=(kc == 0), stop=(kc == KC - 1)),
                      reads=["ones_bf", src_sq_key + str(kc)], writes=["ps%d" % b])
            A("dve", lambda h, b=b: h.tensor_scalar(out=r4[:], in0=PS[b][:, 0:4], scalar1=1.0 / D, scalar2=EPS, op0=ALU.mult, op1=ALU.add),
              reads=["ps%d" % b], writes=["r4"])
            mh4 = bass.AP(mhalf1[:].tensor, mhalf1[:].offset, [list(mhalf1[:].ap[0]), [0, 4]])
            A("pool", lambda h: h.tensor_tensor(out=r4[:], in0=r4[:], in1=mh4, op=ALU.pow), reads=["r4", "mhalf"], writes=["r4"])
            b2 = nb()
            for tc in range(TC):
                A("dve", lambda h, tc=tc: h.tensor_scalar(out=diag[:, tc, :], in0=ident_f[:], scalar1=r4[:, tc:tc + 1], scalar2=None, op0=ALU.mult),
                  reads=["ident_f", "r4"], writes=["diag%d" % tc])
                A("pe", lambda h, tc=tc, b2=b2: h.matmul(PS[b2][:, tc * 128:(tc + 1) * 128], lhsT=ones_f[:], rhs=diag[:, tc, :], start=True, stop=True),
                  reads=["ones_f", "diag%d" % tc], writes=["ps%d" % b2])
            A("act", lambda h, b2=b2: h.activation(out=dst, in_=PS[b2][:], func=AF.Copy), reads=["ps%d" % b2], writes=[dst_key])

        def mod_compute(l):
            for fc in range(6):
                ci, s = ring_use()
                b = nb()
                for kc in range(KC):
                    A("pe", lambda h, kc=kc, b=b, s=s: h.matmul(PS[b][0:1, :], lhsT=cond_bf[:, kc:kc + 1], rhs=ring[s][:, kc * 512:(kc + 1) * 512],
                                                                start=(kc == 0), stop=(kc == KC - 1)),
                      reads=["cond", ] + slotk(s), writes=["ps%d" % b])
                A("act", lambda h, b=b, fc=fc: h.activation(out=modrow[0:1, fc * 512:(fc + 1) * 512], in_=PS[b][0:1, :], func=AF.Copy),
                  reads=["ps%d" % b], writes=sck(16, 6))
                ring_done(ci)
            b = nb()
            for jj in range(24):
                A("pe", lambda h, jj=jj, b=b: h.matmul(PS[b][:, jj:jj + 1], lhsT=modrow[0:1, jj * 128:(jj + 1) * 128], rhs=one_f[0:1, 0:1],
                                                       start=True, stop=True),
                  reads=sck(16, 6) + ["one_f"], writes=["ps%d" % b])
            A("dve", lambda h, b=b: h.tensor_tensor(out=modT[:], in0=PS[b][:, 0:24], in1=ptab[:, P_MODB + l * 24:P_MODB + (l + 1) * 24], op=ALU.add),
              reads=["ps%d" % b, "ptab"], writes=["modT"])
            A("dve", lambda h: h.scalar_tensor_tensor(out=Acoef[:, l, :], in0=modT[:, 8:16], scalar=1.0, in1=ptab[:, P_PRE + l * 8:P_PRE + (l + 1) * 8],
                                                      op0=ALU.add, op1=ALU.mult), reads=["modT", "ptab"], writes=["coef%d" % l])
            A("dve", lambda h: h.tensor_copy(out=Bcoef[:, l, :], in_=modT[:, 0:8]), reads=["modT"], writes=["coef%d" % l])
            A("dve", lambda h: h.tensor_tensor(out=Gcoef[:, l, :], in0=modT[:, 16:24], in1=ptab[:, P_POST + l * 8:P_POST + (l + 1) * 8], op=ALU.mult),
              reads=["modT", "ptab"], writes=["coef%d" % l])

        def pre_norm(l, X, xk):
            for kc in range(KC):
                A("act", lambda h, kc=kc: h.activation(out=yT[:, kc, :], in_=X[:, kc, :], func=AF.Square), reads=[xk + str(kc)], writes=["yT%d" % kc])
            rms_bcast(yT, "yT", rbc[:], "rbc")
            for kc in range(KC):
                t = sc(20 + kc % 2)
                tk = sck(20 + kc % 2)
                A("dve", lambda h, kc=kc, t=t: h.tensor_tensor(out=t, in0=X[:, kc, :], in1=rbc[:], op=ALU.mult), reads=[xk + str(kc), "rbc"], writes=tk)
                A("act", lambda h, kc=kc, t=t: h.activation(out=hT[:, kc, :], in_=t, func=AF.Identity, scale=Acoef[:, l, kc:kc + 1], bias=Bcoef[:, l, kc:kc + 1]),
                  reads=tk + ["coef%d" % l], writes=["hT%d" % kc])

        def out_proj_post(l, X, xk, slots_fn):
            ci = s = None
            for fo in range(KC):
                if fo % 2 == 0:
                    ci, s = ring_use()
                b = nb()
                for kc in range(EC):
                    A("pe", lambda h, kc=kc, b=b, s=s, fo=fo: h.matmul(PS[b][:], lhsT=ring[s][:, kc * 256 + (fo % 2) * 128:kc * 256 + (fo % 2) * 128 + 128],
                                                                       rhs=yT[:, kc, :], start=(kc == 0), stop=(kc == EC - 1)),
                      reads=slotk(s) + ["yT%d" % kc], writes=["ps%d" % b])
                A("act", lambda h, b=b, fo=fo: h.activation(out=oT[:, fo, :], in_=PS[b][:], func=AF.Copy), reads=["ps%d" % b], writes=["oT%d" % fo])
                A("act", lambda h, b=b, fo=fo: h.activation(out=hT[:, fo, :], in_=PS[b][:], func=AF.Square), reads=["ps%d" % b], writes=["hT%d" % fo])
                if fo % 2 == 1:
                    ring_done(ci)
            rms_bcast(hT, "hT", rbc[:], "rbc")
            for fo in range(KC):
                t = sc(20 + fo % 2)
                tk = sck(20 + fo % 2)
                A("dve", lambda h, fo=fo, t=t: h.tensor_tensor(out=t, in0=oT[:, fo, :], in1=rbc[:], op=ALU.mult), reads=["oT%d" % fo, "rbc"], writes=tk)
                A("dve", lambda h, fo=fo, t=t: h.scalar_tensor_tensor(out=X[:, fo, :], in0=t, scalar=Gcoef[:, l, fo:fo + 1], in1=X[:, fo, :],
                                                                      op0=ALU.mult, op1=ALU.add),
                  reads=tk + ["coef%d" % l, xk + str(fo)], writes=[xk + str(fo)])

        def layer_a(l, X, xk):
            j = l // 2
            pre_norm(l, X, xk)
            vslots = [ring_use() for _ in range(4)]
            vn = scr[:, 8 * 512:16 * 512].bitcast(BF16)
            for tc in range(TC):
                gi = tc % 2
                gv = sc(gi * 4, 4)
                for fc in range(4):
                    ci, s = vslots[fc]
                    b = nb()
                    for kc in range(KC):
                        A("pe", lambda h, kc=kc, b=b, s=s, tc=tc: h.matmul(PS[b][:], lhsT=hT[:, kc, tc * 128:(tc + 1) * 128], rhs=ring[s][:, kc * 512:(kc + 1) * 512],
                                                                           start=(kc == 0), stop=(kc == KC - 1)),
                          reads=["hT%d" % kc, ] + slotk(s), writes=["ps%d" % b])
                    A("act", lambda h, b=b, fc=fc, gv=gv: h.activation(out=gv[:, fc * 512:(fc + 1) * 512], in_=PS[b][:], func=AF.Gelu_apprx_tanh),
                      reads=["ps%d" % b], writes=sck(gi * 4 + fc))
                    A("dve", lambda h, fc=fc, gv=gv, gi=gi: h.bn_stats(out=stats[:, gi, fc, :], in_=gv[:, fc * 512:(fc + 1) * 512]),
                      reads=sck(gi * 4 + fc), writes=["stats%d" % gi])
                A("dve", lambda h, gi=gi: h.bn_aggr(out=mv[:, gi, :], in_=stats[:, gi, :, :].rearrange("p a b -> p (a b)")), reads=["stats%d" % gi], writes=["mv%d" % gi])
                A("dve", lambda h, gi=gi: h.tensor_scalar(out=rs1[:, gi, 0:1], in0=mv[:, gi, 1:2], scalar1=EPS, scalar2=None, op0=ALU.add),
                  reads=["mv%d" % gi], writes=["rs%d" % gi])
                A("pool", lambda h, gi=gi: h.tensor_tensor(out=rs1[:, gi, 1:2], in0=rs1[:, gi, 0:1], in1=mhalf1[:], op=ALU.pow),
                  reads=["rs%d" % gi, "mhalf"], writes=["rs%d" % gi])
                A("dve", lambda h, gi=gi, gv=gv, tc=tc: h.tensor_scalar(out=vn[:, tc * 2048:(tc + 1) * 2048], in0=gv, scalar1=mv[:, gi, 0:1], scalar2=rs1[:, gi, 1:2],
                                                                        op0=ALU.subtract, op1=ALU.mult),
                  reads=sck(gi * 4, 4) + ["mv%d" % gi, "rs%d" % gi], writes=sck(8 + tc * 2, 2))
            for (ci, s) in vslots:
                ring_done(ci)

            def mix(cc):
                g = cc // 2
                b = nb()
                for tc in range(TC):
                    A("pe", lambda h, tc=tc, b=b: h.matmul(PS[b][:, tc * 128:(tc + 1) * 128], lhsT=vn[:, tc * 2048 + cc * 128:tc * 2048 + (cc + 1) * 128],
                                                           rhs=wsT[:, j * 8 + g, :], start=True, stop=True),
                      reads=sck(8 + tc * 2, 2) + ["wsT"], writes=["ps%d" % b])
                return b

            pend = None
            uci = gci = us = gs = None
            for cc in range(EC + 1):
                if cc < EC:
                    if cc % 4 == 0:
                        uci, us = ring_use()
                        gci, gs = ring_use()
                    off = (cc % 4) * 128
                    bu = nb()
                    for kc in range(KC):
                        A("pe", lambda h, kc=kc, b=bu, s=us, off=off: h.matmul(PS[b][:], lhsT=ring[s][:, kc * 512 + off:kc * 512 + off + 128], rhs=hT[:, kc, :],
                                                                               start=(kc == 0), stop=(kc == KC - 1)),
                          reads=[] + slotk(us) + ["hT%d" % kc], writes=["ps%d" % bu])
                    bg = nb()
                    for kc in range(KC):
                        A("pe", lambda h, kc=kc, b=bg, s=gs, off=off: h.matmul(PS[b][:], lhsT=ring[s][:, kc * 512 + off:kc * 512 + off + 128], rhs=hT[:, kc, :],
                                                                               start=(kc == 0), stop=(kc == KC - 1)),
                          reads=[] + slotk(gs) + ["hT%d" % kc], writes=["ps%d" % bg])
                    if cc % 4 == 3:
                        ring_done(uci)
                        ring_done(gci)
                    par = cc % 2
                    t_gu, t_th, t_sg, t_m = 16 + par * 2, 17 + par * 2, 0 + par * 2, 1 + par * 2
                    A("act", lambda h, b=bu, t=t_gu: h.activation(out=sc(t), in_=PS[b][:], func=AF.Gelu_apprx_tanh), reads=["ps%d" % bu], writes=sck(t_gu))
                    A("act", lambda h, b=bg, t=t_th: h.activation(out=sc(t), in_=PS[b][:], func=AF.Tanh, scale=0.5), reads=["ps%d" % bg], writes=sck(t_th))
                    A("dve", lambda h, b=bg, t=t_th, o=t_sg: h.scalar_tensor_tensor(out=sc(o), in0=sc(t), scalar=1.0, in1=PS[b][:], op0=ALU.add, op1=ALU.mult),
                      reads=sck(t_th) + ["ps%d" % bg], writes=sck(t_sg))
                    A("dve", lambda h, a=t_gu, o=t_sg: h.tensor_tensor(out=sc(o), in0=sc(a), in1=sc(o), op=ALU.mult), reads=sck(t_gu) + sck(t_sg), writes=sck(t_sg))
                    cur = (cc, t_sg, t_m)
                else:
                    cur = None
                if pend is not None:
                    pcc, p_sg, p_m = pend
                    bm = mix(pcc)
                    g = pcc // 2
                    bs_ap = bass.AP(bS[:].tensor, bS[:, (j * 8 + g) * 128:(j * 8 + g) * 128 + 128].offset, [list(bS[:].ap[0]), [0, 4], [1, 128]])
                    A("dve", lambda h, b=bm, o=p_m, pcc=pcc, bs_ap=bs_ap: h.scalar_tensor_tensor(
                        out=sc(o).rearrange("p (a b) -> p a b", a=4), in0=PS[b][:].rearrange("p (a b) -> p a b", a=4),
                        scalar=ptab[:, P_VN + j * 16 + pcc:P_VN + j * 16 + pcc + 1], in1=bs_ap, op0=ALU.mult, op1=ALU.add),
                      reads=["ps%d" % bm, "ptab", "bS"], writes=sck(p_m))
                    A("dve", lambda h, o=p_m, s_=p_sg, pcc=pcc: h.scalar_tensor_tensor(out=yT[:, pcc, :], in0=sc(o), scalar=0.5, in1=sc(s_), op0=ALU.mult, op1=ALU.mult),
                      reads=sck(p_m) + sck(p_sg), writes=["yT%d" % pcc])
                pend = cur
            out_proj_post(l, X, xk, None)

        def layer_b(l, X, xk):
            lb = l // 2
            pre_norm(l, X, xk)
            def xbuf(hp, ci, a=0, n=515):
                return xbufs[:, hp * 2 + ci, a:a + n]
            xbk = lambda hp, ci: ["xbuf%d%d" % (hp, ci)]
            xbtk = lambda hp, ci: ["xbufT%d%d" % (hp, ci)]
            XC = 0
            XCB = 4
            xcb_all = scr[:, XCB * 512:(XCB + 2) * 512].bitcast(BF16)
            xcbk = lambda hp, ci: ["sc%d%s" % (XCB + (hp * 2 + ci) // 2, "ab"[(hp * 2 + ci) % 2])]

            def T2(hp, jc, w):
                return 6 + (hp * 2 + jc) * 4 + w
            xq = {}
            banks = {}

            def pe_g(hd):
                if hd % 2 == 0:
                    xq["g"] = ring_use()
                gci, gs = xq["g"]
                for ci in range(2):
                    b = nb()
                    banks[("g", hd, ci)] = b
                    off = ((hd % 2) * 2 + ci) * 128
                    for kc in range(KC):
                        A("pe", lambda h, kc=kc, b=b, s=gs, off=off: h.matmul(PS[b][:], lhsT=ring[s][:, kc * 512 + off:kc * 512 + off + 128], rhs=hT[:, kc, :],
                                                                             start=(kc == 0), stop=(kc == KC - 1)),
                          reads=slotk(gs) + ["hT%d" % kc], writes=["ps%d" % b])
                if hd % 2 == 1:
                    ring_done(gci)

            def act_xcb(hd):
                hp = hd % 2
                for ci in range(2):
                    xc = sc(XC + hp * 2 + ci)
                    A("act", lambda h, xc=xc, hp=hp, ci=ci: h.activation(out=xcb_all[:, (hp * 2 + ci) * 512:(hp * 2 + ci + 1) * 512], in_=xc, func=AF.Copy),
                      reads=sck(XC + hp * 2 + ci), writes=xcbk(hp, ci))

            def dve_tail_restore(hd):
                hp = hd % 2
                for ci in range(2):
                    cc = hd * 2 + ci
                    A("dve", lambda h, hp=hp, ci=ci, cc=cc: h.tensor_copy(out=xbuf(hp, ci, 0, 3), in_=ctail[:, lb, cc, 0:3]),
                      reads=["ctail%d_%d" % (lb, cc)], writes=xbtk(hp, ci))

            def act_sqrt(hd):
                hp = hd % 2
                for jc in range(2):
                    t_r, t_a2 = T2(hp, jc, 0), T2(hp, jc, 2)
                    A("act", lambda h, o=t_a2, r=t_r: h.activation(out=sc(r), in_=sc(o), func=AF.Sqrt, scale=-1.0 / 16, bias=1.0 / 16), reads=sck(t_a2), writes=sck(t_r))
                    A("act", lambda h, o=t_a2: h.activation(out=sc(o), in_=sc(o), func=AF.Sqrt), reads=sck(t_a2), writes=sck(t_a2))

            def dve_s3(hd):
                hp = hd % 2
                for jc in range(2):
                    cc = hd * 2 + jc
                    t_r, t_i, t_a2, t_g = T2(hp, jc, 0), T2(hp, jc, 1), T2(hp, jc, 2), T2(hp, jc, 3)
                    A("dve", lambda h, t=t_i, r=t_r: h.tensor_tensor(out=sc(t), in0=sc(t), in1=sc(r), op=ALU.mult), reads=sck(t_i) + sck(t_r), writes=sck(t_i))
                    A("dve", lambda h, o=t_a2, t=t_i, r=t_r, cc=cc: h.tensor_tensor_scan(out=sc(r), data0=sc(o), data1=sc(t), initial=hstate[:, lb, cc:cc + 1],
                                                                                        op0=ALU.mult, op1=ALU.add),
                      reads=sck(t_a2) + sck(t_i) + ["hst%d_%d" % (lb, cc)], writes=sck(t_r))
                    A("dve", lambda h, r=t_r, cc=cc: h.tensor_copy(out=hstate[:, lb, cc:cc + 1], in_=sc(r)[:, 511:512]), reads=sck(t_r), writes=["hst%d_%d" % (lb, cc)])
                    A("dve", lambda h, r=t_r, tg=t_g, cc=cc: h.tensor_tensor(out=yT[:, cc, :], in0=sc(r), in1=sc(tg), op=ALU.mult),
                      reads=sck(t_r) + sck(t_g), writes=["yT%d" % cc])

            def pe_xb_evac(hd):
                hp = hd % 2
                if hd % 2 == 0:
                    xq["x"] = ring_use()
                ci_, s = xq["x"]
                for ci in range(2):
                    b = nb()
                    off = ((hd % 2) * 2 + ci) * 128
                    for kc in range(KC):
                        A("pe", lambda h, kc=kc, b=b, s=s, off=off: h.matmul(PS[b][:], lhsT=ring[s][:, kc * 512 + off:kc * 512 + off + 128], rhs=hT[:, kc, :],
                                                                             start=(kc == 0), stop=(kc == KC - 1)),
                          reads=slotk(s) + ["hT%d" % kc], writes=["ps%d" % b])
                    A("act", lambda h, b=b, hp=hp, ci=ci: h.activation(out=xbuf(hp, ci, 3, 512), in_=PS[b][:], func=AF.Copy),
                      reads=["ps%d" % b], writes=xbk(hp, ci))
                if hd % 2 == 1:
                    ring_done(ci_)

            def act_thg(hd):
                hp = hd % 2
                for ci in range(2):
                    b = banks[("g", hd, ci)]
                    t = T2(hp, ci, 3)
                    A("act", lambda h, b=b, t=t: h.activation(out=sc(t), in_=PS[b][:], func=AF.Tanh, scale=0.5), reads=["ps%d" % b], writes=sck(t))

            def pe_gates_E(hd):
                hp = hd % 2
                h2 = hd % 2
                if hd % 2 == 0:
                    xq["w"] = ring_use()
                wci, ws = xq["w"]
                for jc in range(2):
                    cc = hd * 2 + jc
                    br = nb()
                    bi = nb()
                    for gi, b in ((0, br), (1, bi)):
                        for ic in range(2):
                            woff = ((h2 * 2 + gi) * 2 + ic) * 256 + jc * 128
                            A("pe", lambda h, b=b, s=ws, woff=woff, ic=ic, hp=hp: h.matmul(PS[b][:], lhsT=ring[s][:, woff:woff + 128],
                                                                                          rhs=xcb_all[:, (hp * 2 + ic) * 512:(hp * 2 + ic + 1) * 512],
                                                                                          start=(ic == 0), stop=(ic == 1)),
                              reads=slotk(ws) + xcbk(hp, ic), writes=["ps%d" % b])
                    t_r, t_i, t_a2 = T2(hp, jc, 0), T2(hp, jc, 1), T2(hp, jc, 2)
                    kcol = Kc[:, lb, cc:cc + 1]
                    A("act", lambda h, b=br, t=t_r, cc=cc: h.activation(out=sc(t), in_=PS[b][:], func=AF.Tanh, scale=0.5, bias=gabh[:, lb, cc:cc + 1]),
                      reads=["ps%d" % br, "gabh"], writes=sck(t_r))
                    A("act", lambda h, b=bi, t=t_i, cc=cc: h.activation(out=sc(t), in_=PS[b][:], func=AF.Tanh, scale=0.5, bias=gxbh[:, lb, cc:cc + 1]),
                      reads=["ps%d" % bi, "gxbh"], writes=sck(t_i))
                    A("act", lambda h, t=t_r, o=t_a2, kcol=kcol: h.activation(out=sc(o), in_=sc(t), func=AF.Exp, scale=kcol, bias=kcol),
                      reads=sck(t_r) + ["Kc"], writes=sck(t_a2))
                if hd % 2 == 1:
                    ring_done(wci)

            def dve_s2(hd):
                hp = hd % 2
                for jc in range(2):
                    t_i = T2(hp, jc, 1)
                    xc = sc(XC + hp * 2 + jc)
                    A("dve", lambda h, t=t_i, xc=xc: h.scalar_tensor_tensor(out=sc(t), in0=sc(t), scalar=1.0, in1=xc, op0=ALU.add, op1=ALU.mult),
                      reads=sck(t_i) + sck(XC + hp * 2 + jc), writes=sck(t_i))
                for ci in range(2):
                    b = banks[("g", hd, ci)]
                    t = T2(hp, ci, 3)
                    A("dve", lambda h, b=b, t=t: h.scalar_tensor_tensor(out=sc(t), in0=sc(t), scalar=1.0, in1=PS[b][:], op0=ALU.add, op1=ALU.mult),
                      reads=sck(t) + ["ps%d" % b], writes=sck(t))

            def dve_conv(hd):
                hp = hd % 2
                for ci in range(2):
                    cc = hd * 2 + ci
                    A("dve", lambda h, hp=hp, ci=ci, cc=cc: h.tensor_copy(out=ctail[:, lb, cc, 0:3], in_=xbuf(hp, ci, 512, 3)),
                      reads=xbk(hp, ci), writes=["ctail%d_%d" % (lb, cc)])
                    xc = sc(XC + hp * 2 + ci)
                    xck = sck(XC + hp * 2 + ci)
                    cw = lambda k, cc=cc: ptab[:, P_CW + (lb * 4 + k) * 16 + cc:P_CW + (lb * 4 + k) * 16 + cc + 1]
                    A("dve", lambda h, xc=xc, hp=hp, ci=ci, cw=cw, cc=cc: h.tensor_scalar(out=xc, in0=xbuf(hp, ci, 3, 512), scalar1=cw(3),
                                                                                         scalar2=ptab[:, P_CB + lb * 16 + cc:P_CB + lb * 16 + cc + 1],
                                                                                         op0=ALU.mult, op1=ALU.add),
                      reads=xbk(hp, ci) + ["ptab"], writes=xck)
                    for k in (2, 1, 0):
                        A("dve", lambda h, k=k, xc=xc, hp=hp, ci=ci, cw=cw: h.scalar_tensor_tensor(out=xc, in0=xbuf(hp, ci, k, 512), scalar=cw(k), in1=xc,
                                                                                                   op0=ALU.mult, op1=ALU.add),
                          reads=xbk(hp, ci) + xbtk(hp, ci) + xck + ["ptab"], writes=xck)

            for k in range(8 + 2):
                if 0 <= k - 1 < 8:
                    pe_g(k - 1)
                if k < 8:
                    dve_tail_restore(k)
                if k - 2 >= 0:
                    act_sqrt(k - 2)
                if 0 <= k - 1 < 8:
                    act_xcb(k - 1)
                if k - 2 >= 0:
                    dve_s3(k - 2)
                if k < 8:
                    pe_xb_evac(k)
                if 0 <= k - 1 < 8:
                    act_thg(k - 1)
                    pe_gates_E(k - 1)
                    dve_s2(k - 1)
                if k < 8:
                    dve_conv(k)
            out_proj_post(l, X, xk, None)

        x_loads = {}

        def load_x(ti):
            X = xT[ti % 2]
            xk = "x%d_" % (ti % 2)
            x_loads[ti] = A("sp", lambda h, ti=ti, X=X: h.dma_start(out=X[:], in_=xT_d[:, ti * T:(ti + 1) * T].rearrange("(k p) t -> p k t", p=128)),
                            writes=[xk + str(kc) for kc in range(KC)], dma_key="xin%d" % (ti % 2))

        out_ops = []
        load_x(0)
        for ti in range(ntiles):
            X = xT[ti % 2]
            xk = "x%d_" % (ti % 2)
            if ti + 1 < ntiles:
                load_x(ti + 1)
            for l in layers:
                if ti == 0:
                    mod_compute(l)
                if l % 2 == 0:
                    layer_a(l, X, xk)
                else:
                    layer_b(l, X, xk)
            out_ops.append(A("sp", lambda h, ti=ti, X=X: h.dma_start(out=oT_d[:, ti * T:(ti + 1) * T].rearrange("(k p) t -> p k t", p=128), in_=X[:]),
                             reads=[xk + str(kc) for kc in range(KC)], dma_key="xout%d" % (ti % 2)))
        assert rstate["next_use"] == len(stream), (rstate, len(stream))
        P.emit(final_waits=out_ops)
    return nc


_CACHE = {}


def _get_nc(layers, ntiles):
    key = (tuple(layers), ntiles)
    if key not in _CACHE:
        _CACHE[key] = build(layers, ntiles)
    return _CACHE[key]


def make_in_maps(inp, batches, ntiles, x_override=None):
    f = lambda a: np.ascontiguousarray(np.asarray(a, dtype=np.float32))
    rows = [f(inp["pre_norm"]).reshape(32, 128), f(inp["post_norm"]).reshape(32, 128), f(inp["mod_b"]).reshape(96, 128),
            f(inp["a_v_norm"]).reshape(32, 128), f(inp["b_conv_w"]).reshape(128, 128), f(inp["b_conv_b"]).reshape(32, 128),
            f(inp["b_ga_b"]).reshape(32, 128), f(inp["b_gx_b"]).reshape(32, 128), f(inp["b_lambda"]).reshape(32, 128)]
    shared = {
        "bs": f(inp["a_b_s"]).reshape(1, 2048),
        "wsT": np.ascontiguousarray(np.transpose(f(inp["a_w_s"]), (0, 1, 3, 2))),
        "mod_w": f(inp["mod_w"]), "a_w_in": f(inp["a_w_in"]), "a_w_out": f(inp["a_w_out"]),
        "b_w_in": f(inp["b_w_in"]), "b_ga_w": f(inp["b_ga_w"]), "b_gx_w": f(inp["b_gx_w"]), "b_w_out": f(inp["b_w_out"]),
    }
    S = ntiles * T
    maps = []
    for bi, b in enumerate(batches):
        par = np.concatenate(rows + [f(inp["c"])[b].reshape(8, 128)], axis=0)
        xb = f(inp["x"])[b] if x_override is None else x_override[bi]
        m = dict(shared)
        m["par"] = np.ascontiguousarray(par.T)
        m["xT"] = np.ascontiguousarray(xb[:S].T)
        maps.append(m)
    return maps


def kernel(**inputs):
    nb_ = 8
    nc = _get_nc((0, 1, 2, 3), SEQ // T)
    maps = make_in_maps(inputs, list(range(nb_)), SEQ // T)
    res = run_bass_kernel_spmd(nc, maps, core_ids=list(range(nb_)))
    out = np.stack([np.ascontiguousarray(r["oT"].T) for r in res.results], axis=0)
    return out.astype(np.float32)
```
